# Optimizing a Trainium2 kernel written in Bass

```python
import math
import jax, jax.numpy as jnp
from jax import lax
import numpy as np

D_MODEL = 2048
BATCH = 2
SEQ = 4096
DEPTH = 1
DEC_BATCH = 32
DEC_SEQ = 64
PAST_LEN = 2048

CHUNK = 64
H_A = 8
DK_A = 64
DV_A = 2 * DK_A
W_QK_A = H_A * 2 * DK_A
W_A = H_A * DV_A
H_B = 8
DH_B = 128
W_B = H_B * DH_B
N_PREV = 8
BAND = N_PREV + 1
BAND_ROWS = N_PREV * CHUNK
MAX_REL = 128
FFN_DIM = 4 * D_MODEL
ROPE_THETA = 10000.0
EPS = 1e-6
Q_BLOCK = 128
NEG = -1e30

kernel_name = 'hybrid_streaming_encoder_step'


def rms_norm(x, g):
    xf = x.astype(jnp.float32)
    y = xf * lax.rsqrt(jnp.mean(xf * xf, axis=-1, keepdims=True) + EPS)
    return (y * g.astype(jnp.float32)).astype(x.dtype)


def swiglu_ffn(x, g, w_gate, w_up, w_down):
    h = rms_norm(x, g)
    return (jax.nn.silu(h @ w_gate) * (h @ w_up)) @ w_down


def rope(x, pos):
    half = x.shape[-1] // 2
    inv = ROPE_THETA ** (-jnp.arange(half, dtype=jnp.float32) / half)
    ang = pos.astype(jnp.float32)[:, None] * inv[None, :]
    cos = jnp.cos(ang)[:, None, :]
    sin = jnp.sin(ang)[:, None, :]
    x1 = x[..., :half].astype(jnp.float32)
    x2 = x[..., half:].astype(jnp.float32)
    out = jnp.concatenate([x1 * cos - x2 * sin, x2 * cos + x1 * sin], axis=-1)
    return out.astype(x.dtype)


def split_projection(h, w_in, pos):
    b, s, _ = h.shape
    z = h @ w_in
    cuts = [W_QK_A, 2 * W_QK_A, 2 * W_QK_A + W_A, 2 * W_QK_A + W_A + W_B,
            2 * W_QK_A + W_A + 2 * W_B, 2 * W_QK_A + W_A + 3 * W_B,
            2 * W_QK_A + W_A + 3 * W_B + D_MODEL]
    qa, ka, va, qb, kb, vb, ga, gb = jnp.split(z, cuts, axis=-1)
    qa = rope(qa.reshape(b, s, 2 * H_A, DK_A), pos).reshape(b, s, H_A, 2, DK_A)
    ka = rope(ka.reshape(b, s, 2 * H_A, DK_A), pos).reshape(b, s, H_A, 2 * DK_A)
    va = va.reshape(b, s, H_A, DV_A)
    qb = qb.reshape(b, s, H_B, DH_B)
    kb = kb.reshape(b, s, H_B, DH_B)
    vb = vb.reshape(b, s, H_B, DH_B)
    return qa, ka, va, qb, kb, vb, ga, gb


def diff_attn_core(q, k, v, mask, lam):
    k = k.reshape(k.shape[:3] + (2, DK_A))
    s = jnp.einsum('bqhmd,bkhmd->bhmqk', q, k).astype(jnp.float32) * (DK_A ** -0.5)
    if mask is not None:
        s = jnp.where(mask, s, NEG)
    p = jax.nn.softmax(s, axis=-1)
    a = p[:, :, 0] - lam * p[:, :, 1]
    return jnp.einsum('bhqk,bkhd->bqhd', a.astype(v.dtype), v)


def diff_attn_prompt(q, k, v, lam):
    b, s = q.shape[:2]
    nb = s // Q_BLOCK
    qb = q.reshape(b, nb, Q_BLOCK, H_A, 2, DK_A).swapaxes(0, 1)
    key_chunk = jnp.arange(s) // CHUNK

    def block(args):
        qi, i = args
        q_chunk = (i * Q_BLOCK + jnp.arange(Q_BLOCK)) // CHUNK
        mask = key_chunk[None, :] <= q_chunk[:, None]
        return diff_attn_core(qi, k, v, mask, lam)

    o = lax.map(block, (qb, jnp.arange(nb)))
    return o.swapaxes(0, 1).reshape(b, s, H_A, DV_A)


def rel_bias_lookup(table, q_pos, k_pos):
    rel = jnp.clip(q_pos[:, None] - k_pos[None, :], -MAX_REL, MAX_REL) + MAX_REL
    return table[:, rel].astype(jnp.float32)


def band_attn_prompt(q, k, v, table):
    b, s = q.shape[:2]
    nc = s // CHUNK
    pad = jnp.zeros((b, N_PREV, CHUNK, H_B, DH_B), k.dtype)
    kp = jnp.concatenate([pad, k.reshape(b, nc, CHUNK, H_B, DH_B)], axis=1)
    vp = jnp.concatenate([pad, v.reshape(b, nc, CHUNK, H_B, DH_B)], axis=1)
    idx = jnp.arange(nc)[:, None] + jnp.arange(BAND)[None, :]
    kband = kp[:, idx].reshape(b, nc, BAND * CHUNK, H_B, DH_B)
    vband = vp[:, idx].reshape(b, nc, BAND * CHUNK, H_B, DH_B)
    valid = jnp.repeat(idx >= N_PREV, CHUNK, axis=1)
    bias = rel_bias_lookup(table, BAND_ROWS + jnp.arange(CHUNK), jnp.arange(BAND * CHUNK))
    qc = q.reshape(b, nc, CHUNK, H_B, DH_B)
    sc = jnp.einsum('bcqhd,bckhd->bchqk', qc, kband).astype(jnp.float32) * (DH_B ** -0.5)
    sc = jnp.where(valid[None, :, None, None, :], sc + bias[None, None], NEG)
    p = jax.nn.softmax(sc, axis=-1)
    o = jnp.einsum('bchqk,bckhd->bcqhd', p.astype(v.dtype), vband)
    return o.reshape(b, s, H_B, DH_B)


def band_attn_sample(q, k_all, v_all, table, past_rows):
    dec = q.shape[1]
    bias = rel_bias_lookup(table, past_rows + jnp.arange(dec), jnp.arange(past_rows + dec))
    sc = jnp.einsum('bqhd,bkhd->bhqk', q, k_all).astype(jnp.float32) * (DH_B ** -0.5)
    p = jax.nn.softmax(sc + bias[None], axis=-1)
    return jnp.einsum('bhqk,bkhd->bqhd', p.astype(v_all.dtype), v_all)


def merge_branches(oa, ob, ga, gb, lam_init, subln, w_branch_a, w_branch_b, w_out):
    b, s = oa.shape[:2]
    oa = (rms_norm(oa, subln) * (1.0 - lam_init)).reshape(b, s, W_A)
    ob = ob.reshape(b, s, W_B)
    merged = jax.nn.sigmoid(ga) * (oa @ w_branch_a) + jax.nn.sigmoid(gb) * (ob @ w_branch_b)
    return merged @ w_out


def setup_inputs(seed: int = 0) -> dict:
    key = jax.random.key(seed)
    ks = jax.random.split(key, 26)
    f32 = jnp.float32

    def w(k, shape, fan_in):
        return jax.random.normal(k, shape, f32) * (fan_in ** -0.5)

    def gain(k, shape):
        return 1.0 + 0.01 * jax.random.normal(k, shape, f32)

    band_cache = min(BAND_ROWS, PAST_LEN)
    in_cols = 2 * W_QK_A + W_A + 3 * W_B + 2 * D_MODEL
    return {
        'x_prompt': jax.random.normal(ks[0], (BATCH, SEQ, D_MODEL), f32),
        'x_sample': jax.random.normal(ks[1], (DEC_BATCH, DEC_SEQ, D_MODEL), f32),
        'cache_a_k': jax.random.normal(ks[2], (DEPTH, DEC_BATCH, PAST_LEN, H_A, 2 * DK_A), f32),
        'cache_a_v': jax.random.normal(ks[3], (DEPTH, DEC_BATCH, PAST_LEN, H_A, DV_A), f32),
        'cache_b_k': jax.random.normal(ks[4], (DEPTH, DEC_BATCH, band_cache, H_B, DH_B), f32),
        'cache_b_v': jax.random.normal(ks[5], (DEPTH, DEC_BATCH, band_cache, H_B, DH_B), f32),
        'ffn1_norm': gain(ks[6], (DEPTH, D_MODEL)),
        'ffn1_w_gate': w(ks[7], (DEPTH, D_MODEL, FFN_DIM), D_MODEL),
        'ffn1_w_up': w(ks[8], (DEPTH, D_MODEL, FFN_DIM), D_MODEL),
        'ffn1_w_down': w(ks[9], (DEPTH, FFN_DIM, D_MODEL), FFN_DIM),
        'mix_norm': gain(ks[10], (DEPTH, D_MODEL)),
        'w_in': w(ks[11], (DEPTH, D_MODEL, in_cols), D_MODEL),
        'lambda_q1': 0.1 * jax.random.normal(ks[12], (DEPTH, DK_A), f32),
        'lambda_k1': 0.1 * jax.random.normal(ks[13], (DEPTH, DK_A), f32),
        'lambda_q2': 0.1 * jax.random.normal(ks[14], (DEPTH, DK_A), f32),
        'lambda_k2': 0.1 * jax.random.normal(ks[15], (DEPTH, DK_A), f32),
        'subln_a': gain(ks[16], (DEPTH, DV_A)),
        'rel_bias_b': 0.2 * jax.random.normal(ks[17], (DEPTH, H_B, 2 * MAX_REL + 1), f32),
        'w_branch_a': w(ks[18], (DEPTH, W_A, D_MODEL), W_A),
        'w_branch_b': w(ks[19], (DEPTH, W_B, D_MODEL), W_B),
        'w_out': w(ks[20], (DEPTH, D_MODEL, D_MODEL), D_MODEL),
        'ffn2_norm': gain(ks[21], (DEPTH, D_MODEL)),
        'ffn2_w_gate': w(ks[22], (DEPTH, D_MODEL, FFN_DIM), D_MODEL),
        'ffn2_w_up': w(ks[23], (DEPTH, D_MODEL, FFN_DIM), D_MODEL),
        'ffn2_w_down': w(ks[24], (DEPTH, FFN_DIM, D_MODEL), FFN_DIM),
        'final_norm': gain(ks[25], (D_MODEL,)),
    }


def reference(x_prompt, x_sample, cache_a_k, cache_a_v, cache_b_k, cache_b_v,
              ffn1_norm, ffn1_w_gate, ffn1_w_up, ffn1_w_down,
              mix_norm, w_in, lambda_q1, lambda_k1, lambda_q2, lambda_k2, subln_a,
              rel_bias_b, w_branch_a, w_branch_b, w_out,
              ffn2_norm, ffn2_w_gate, ffn2_w_up, ffn2_w_down, final_norm):
    seq = x_prompt.shape[1]
    dec_seq = x_sample.shape[1]
    past_a = cache_a_k.shape[2]
    past_b = cache_b_k.shape[2]
    prompt_band_rows = min(BAND_ROWS, seq)
    pos_prompt = jnp.arange(seq)
    pos_sample = past_a + jnp.arange(dec_seq)
    f32 = jnp.float32

    xp, xs = x_prompt, x_sample
    ak_p, av_p, bk_p, bv_p = [], [], [], []
    ak_s, av_s, bk_s, bv_s = [], [], [], []
    for layer in range(DEPTH):
        lam_init = 0.8 - 0.6 * math.exp(-0.3 * layer)
        lam = (jnp.exp(jnp.sum(lambda_q1[layer].astype(f32) * lambda_k1[layer].astype(f32)))
               - jnp.exp(jnp.sum(lambda_q2[layer].astype(f32) * lambda_k2[layer].astype(f32)))
               + lam_init)
        ffn1 = (ffn1_norm[layer], ffn1_w_gate[layer], ffn1_w_up[layer], ffn1_w_down[layer])
        ffn2 = (ffn2_norm[layer], ffn2_w_gate[layer], ffn2_w_up[layer], ffn2_w_down[layer])
        merge_w = (subln_a[layer], w_branch_a[layer], w_branch_b[layer], w_out[layer])

        xp = xp + 0.5 * swiglu_ffn(xp, *ffn1)
        qa, ka, va, qb, kb, vb, ga, gb = split_projection(rms_norm(xp, mix_norm[layer]), w_in[layer], pos_prompt)
        oa = diff_attn_prompt(qa, ka, va, lam)
        ob = band_attn_prompt(qb, kb, vb, rel_bias_b[layer])
        xp = xp + merge_branches(oa, ob, ga, gb, lam_init, *merge_w)
        xp = xp + 0.5 * swiglu_ffn(xp, *ffn2)
        ak_p.append(ka)
        av_p.append(va)
        bk_p.append(kb[:, seq - prompt_band_rows:])
        bv_p.append(vb[:, seq - prompt_band_rows:])

        xs = xs + 0.5 * swiglu_ffn(xs, *ffn1)
        qa, ka, va, qb, kb, vb, ga, gb = split_projection(rms_norm(xs, mix_norm[layer]), w_in[layer], pos_sample)
        ka_all = jnp.concatenate([cache_a_k[layer], ka], axis=1)
        va_all = jnp.concatenate([cache_a_v[layer], va], axis=1)
        oa = diff_attn_core(qa, ka_all, va_all, None, lam)
        kb_all = jnp.concatenate([cache_b_k[layer], kb], axis=1)
        vb_all = jnp.concatenate([cache_b_v[layer], vb], axis=1)
        ob = band_attn_sample(qb, kb_all, vb_all, rel_bias_b[layer], past_b)
        xs = xs + merge_branches(oa, ob, ga, gb, lam_init, *merge_w)
        xs = xs + 0.5 * swiglu_ffn(xs, *ffn2)
        ak_s.append(ka)
        av_s.append(va)
        bk_s.append(kb_all[:, dec_seq:])
        bv_s.append(vb_all[:, dec_seq:])

    y_prompt = rms_norm(xp, final_norm)
    y_sample = rms_norm(xs, final_norm)
    return (y_prompt, y_sample,
            jnp.stack(ak_p), jnp.stack(av_p), jnp.stack(bk_p), jnp.stack(bv_p),
            jnp.stack(ak_s), jnp.stack(av_s), jnp.stack(bk_s), jnp.stack(bv_s))
```

```python
import numpy as np
import concourse.bass as bass
import concourse.mybir as mybir
from concourse.bass_utils import run_bass_kernel_spmd

F32 = mybir.dt.float32
BF16 = mybir.dt.bfloat16
AF = mybir.ActivationFunctionType
ALU = mybir.AluOpType
AX = mybir.AxisListType

D = 2048
FF = 8192
NCH = 16
TP = 1024
TSM = 64
TT = 1024
NTILES = 4
SUB = [(0, 512), (512, 512), (1024, 64)]
TG = [(g * 128, 128) for g in range(8)] + [(1024, 64)]
EPS = 1e-6
LAM_INIT = 0.2
NEG_BIG = -300.0
C_QA, C_KA, C_VA, C_QB, C_KB, C_VB, C_GA, C_GB = 0, 1024, 2048, 3072, 4096, 5120, 6144, 8192
NCORES = 8


class Op:
    __slots__ = ("eng", "fn", "deps", "dma", "sig", "epoch")

    def __init__(self, eng, fn, deps, dma, epoch):
        self.eng, self.fn, self.deps, self.dma, self.epoch = eng, fn, deps, dma, epoch
        self.sig = None


class Sched:
    ENGS = ("pe", "act", "dve", "pool", "sp")

    def __init__(self):
        self.ops = []
        self.last_w = {}
        self.readers = {}
        self.epoch = 0
        self.barrier_deps = {e: set() for e in self.ENGS}
        self.last_on_eng = {}
        self.last_on_stream = {}

    def add(self, eng, fn, r=(), w=(), dma=None):
        deps = set()
        for k in r:
            if k in self.last_w:
                deps.add(self.last_w[k])
        for k in w:
            if k in self.last_w:
                deps.add(self.last_w[k])
            deps.update(self.readers.get(k, ()))
        if self.barrier_deps[eng]:
            deps.update(self.barrier_deps[eng])
            self.barrier_deps[eng] = set()
        idx = len(self.ops)
        self.ops.append(Op(eng, fn, deps, dma, self.epoch))
        for k in r:
            self.readers.setdefault(k, []).append(idx)
        for k in w:
            self.last_w[k] = idx
            self.readers[k] = []
        if dma is None:
            self.last_on_eng[eng] = idx
        else:
            self.last_on_stream[dma] = idx
        return idx

    def barrier(self, new_epoch=False):
        s = set(self.last_on_eng.values()) | set(self.last_on_stream.values())
        for e in self.ENGS:
            self.barrier_deps[e] = set(s)
        if new_epoch:
            self.epoch += 1

    def emit(self, nc, final_streams):
        ops = self.ops
        need = set()
        for op in ops:
            need.update(op.deps)
        streams = sorted({op.dma for op in ops if op.dma is not None})
        nep = self.epoch + 1
        import contextlib
        es = contextlib.ExitStack()
        sem_stream = {s: es.enter_context(nc.semaphore("st_" + s)) for s in streams}
        sem_eng = {(e, ep): es.enter_context(nc.semaphore("e_%s_%d" % (e, ep)))
                   for e in ("pe", "act", "dve") for ep in range(nep)}
        block = es.enter_context(nc.Block())
        cnt_s = {s: 0 for s in streams}
        cnt_e = {k: 0 for k in sem_eng}
        for i, op in enumerate(ops):
            if op.dma is not None:
                cnt_s[op.dma] += 16
                op.sig = (sem_stream[op.dma], cnt_s[op.dma], 16)
            elif i in need:
                k = (op.eng, op.epoch)
                cnt_e[k] += 1
                op.sig = (sem_eng[k], cnt_e[k], 1)
        per = {e: [] for e in self.ENGS}
        for i, op in enumerate(ops):
            per[op.eng].append(i)

        def run(eng_name, h):
            waited = {}
            for i in per[eng_name]:
                op = ops[i]
                wl = {}
                for d in op.deps:
                    dop = ops[d]
                    if dop.dma is None and dop.eng == eng_name and eng_name == "pe":
                        continue
                    sem, val, _ = dop.sig
                    key = id(sem)
                    if waited.get(key, 0) >= val:
                        continue
                    if key not in wl or wl[key][1] < val:
                        wl[key] = (sem, val)
                for key, (sem, val) in wl.items():
                    h.wait_ge(sem, val)
                    waited[key] = val
                ins = op.fn(h)
                if op.sig is not None:
                    assert ins is not None, (eng_name, i)
                    ins.then_inc(op.sig[0], op.sig[2])
            if eng_name == "sp":
                for s in final_streams:
                    if s in cnt_s and cnt_s[s] > 0:
                        h.wait_ge(sem_stream[s], cnt_s[s])

        @block.tensor
        def _(h):
            run("pe", h)

        @block.scalar
        def _(h):
            run("act", h)

        @block.vector
        def _(h):
            run("dve", h)

        @block.gpsimd
        def _(h):
            run("pool", h)

        @block.sync
        def _(h):
            run("sp", h)
        return es


def build_program():
    nc = bass.Bass("TRN2", target_bir_lowering=False)
    S = Sched()

    def din(name, shape, dt=F32):
        return nc.dram_tensor(name, list(shape), dt, kind="ExternalInput").ap()

    def dout(name, shape, dt=F32):
        return nc.dram_tensor(name, list(shape), dt, kind="ExternalOutput").ap()

    def dscr(name, shape, dt):
        return nc.dram_tensor(name, list(shape), dt, kind="Internal")

    xpre = din("xpre", [3, 1024, D]); xown = din("xown", [1024, D]); xs = din("xs", [4, 64, D])
    cak = din("cak", [4, 2048, 1024]); cav = din("cav", [4, 2048, 1024])
    cbk = din("cbk", [4, 512, 1024]); cbv = din("cbv", [4, 512, 1024])
    f1g = din("f1g", [D, FF]); f1u = din("f1u", [D, FF]); f1d = din("f1d", [FF, D])
    f2g = din("f2g", [D, FF]); f2u = din("f2u", [D, FF]); f2d = din("f2d", [FF, D])
    win = din("win", [D, 10240]); wba = din("wba", [1024, D]); wbb = din("wbb", [1024, D]); wout = din("wout", [D, D])
    lamv = din("lamv", [4, 128, 64]); subln = din("subln", [128, 1]); relb = din("relb", [8, 257])
    cosP = din("cosP", [3, 128, 1024]); sinP = din("sinP", [3, 128, 1024])
    cosO = din("cosO", [128, 1024]); sinO = din("sinO", [128, 1024])
    cosS = din("cosS", [128, 256]); sinS = din("sinS", [128, 256])
    kbias_in = din("kbias", [128, 8])
    yp = dout("yp", [1024, D]); ys = dout("ys", [4, 64, D])
    akp = dout("akp", [1024, 1024]); avp = dout("avp", [1024, 1024])
    bkp = dout("bkp", [512, 1024]); bvp = dout("bvp", [512, 1024])
    aks = dout("aks", [4, 64, 1024]); avs = dout("avs", [4, 64, 1024])
    bks = dout("bks", [4, 512, 1024]); bvs = dout("bvs", [4, 512, 1024])
    kaT_s = dscr("kaT_s", [8, 128, 4096], BF16).ap()
    kbT_s = dscr("kbT_s", [8, 128, 4096], BF16).ap()
    va_s = dscr("va_s", [8, 128, 32, 128], BF16).ap()
    vb_s = dscr("vb_s", [8, 128, 32, 128], BF16).ap()
    tbx_t = dscr("tbx", [8, 128, 1024], F32)
    wscr = {
        "f1g": dscr("f1g_s", [32, 128, 4096], BF16).ap(), "f1u": dscr("f1u_s", [32, 128, 4096], BF16).ap(),
        "f1d": dscr("f1d_s", [32, 128, 4096], BF16).ap(),
        "f2g": dscr("f2g_s", [32, 128, 4096], BF16).ap(), "f2u": dscr("f2u_s", [32, 128, 4096], BF16).ap(),
        "f2d": dscr("f2d_s", [32, 128, 4096], BF16).ap(),
        "wi": dscr("wi_s", [24, 128, 4096], BF16).ap(),
        "wga": dscr("wga_s", [16, 128, 2048], BF16).ap(), "wgb": dscr("wgb_s", [16, 128, 2048], BF16).ap(),
        "wa": dscr("wa_s", [16, 128, 1024], BF16).ap(), "wb": dscr("wb_s", [16, 128, 1024], BF16).ap(),
        "wo": dscr("wo_s", [8, 128, 4096], BF16).ap(),
    }
    tbx = tbx_t.ap()

    import contextlib
    stack = contextlib.ExitStack()
    BIGW = 53200
    big = stack.enter_context(nc.sbuf_tensor("big", [128, BIGW], F32))
    banks = [stack.enter_context(nc.psum_tensor("bank%d" % i, [128, 512], F32)) for i in range(8)]

    class Arena:
        def __init__(self, base, limit):
            self.p, self.limit = base, limit

        def f32(self, n):
            o = self.p; self.p += n
            assert self.p <= self.limit, (self.p, self.limit)
            return big[:, o:o + n]

        def bf(self, n):
            n4 = (n + 1) // 2
            o = self.p; self.p += n4
            assert self.p <= self.limit, (self.p, self.limit)
            return big[:, o:o + n4].bitcast(BF16)[:, 0:n]

    A0 = Arena(0, BIGW)
    x_flat = A0.f32(NCH * TT)
    x = x_flat.rearrange("p (c t) -> p c t", c=NCH)
    identF = A0.f32(128)
    identB = A0.bf(128)
    onesB = A0.bf(128)
    onesF = A0.f32(128)
    gsb = A0.f32(4 * NCH).rearrange("p (g c) -> p g c", g=4)
    gsub = A0.f32(1)
    neglam = A0.f32(1)
    epsT = A0.f32(1)
    rstd_mix = A0.f32(TT)
    ksTa = A0.bf(8 * 256).rearrange("p (h t) -> p h t", h=8)
    ksTb = A0.bf(8 * 256).rearrange("p (h t) -> p h t", h=8)
    vsa = A0.bf(4 * 1024).rearrange("p (b f) -> p b f", b=4)
    vsb = A0.bf(4 * 1024).rearrange("p (b f) -> p b f", b=4)
    kbias = A0.f32(8)
    lamtmp = A0.f32(4 * 64).rearrange("p (a b) -> p a b", a=4)
    lamr = A0.f32(4)
    PBASE = A0.p
    QO = Arena(PBASE, BIGW)
    qa = QO.bf(8 * TT).rearrange("p (c t) -> p c t", c=8)
    qb = QO.bf(8 * TT).rearrange("p (c t) -> p c t", c=8)
    QBASE = QO.p

    def phase_arena(keep_qo=False):
        return Arena(QBASE if keep_qo else PBASE, BIGW)

    def PE(fn, r, w): return S.add("pe", fn, r, w)
    def ACT(fn, r, w): return S.add("act", fn, r, w)
    def DVE(fn, r, w): return S.add("dve", fn, r, w)
    def DMA_P(fn, r, w, stream): return S.add("pool", fn, r, w, dma=stream)
    def DMA_S(fn, r, w, stream): return S.add("sp", fn, r, w, dma=stream)
    def DMA_A(fn, r, w, stream): return S.add("act", fn, r, w, dma=stream)
    out_streams = set()

    bank_rr = [0]

    def next_bank(pool):
        b = pool[bank_rr[0] % len(pool)]
        bank_rr[0] += 1
        return b

    def init_consts():
        DVE(lambda h: h.memset(onesF, 1.0), [], ["onesF"])
        DVE(lambda h: h.memset(onesB, 1.0), [], ["onesB"])
        DVE(lambda h: h.memset(epsT, EPS), [], ["epsT"])
        DMA_S(lambda h: h.dma_start(out=identF, in_=ident_in), [], ["identF"], "cst")
        for g in range(4):
            DMA_S(lambda h, g=g: h.dma_start(out=gsb[:, g, :], in_=gains_r[g]), [], [("gsb", g)], "cst")
        DMA_S(lambda h: h.dma_start(out=gsub, in_=subln), [], ["gsub"], "cst")
        DMA_S(lambda h: h.dma_start(out=kbias, in_=kbias_in), [], ["kbias"], "cst")
        for a in range(4):
            DMA_S(lambda h, a=a: h.dma_start(out=lamtmp[:, a, :], in_=lamv[a]), [], [("lamtmp", a)], "cst")
        S.barrier()
        DVE(lambda h: h.tensor_copy(out=identB, in_=identF), ["identF"], ["identB"])
        DVE(lambda h: h.tensor_scalar(out=gsub, in0=gsub, scalar1=1.0 - LAM_INIT, scalar2=0.0, op0=ALU.mult, op1=ALU.add),
            ["gsub"], ["gsub"])
        DVE(lambda h: h.tensor_tensor(out=lamtmp[:, 0, :], in0=lamtmp[:, 0, :], in1=lamtmp[:, 1, :], op=ALU.mult),
            [("lamtmp", 0), ("lamtmp", 1)], [("lamtmp", 0)])
        DVE(lambda h: h.tensor_tensor(out=lamtmp[:, 2, :], in0=lamtmp[:, 2, :], in1=lamtmp[:, 3, :], op=ALU.mult),
            [("lamtmp", 2), ("lamtmp", 3)], [("lamtmp", 2)])
        DVE(lambda h: h.reduce_sum(out=lamr[:, 0:1], in_=lamtmp[:, 0, :], axis=AX.X), [("lamtmp", 0)], [("lamr", 0)])
        DVE(lambda h: h.reduce_sum(out=lamr[:, 1:2], in_=lamtmp[:, 2, :], axis=AX.X), [("lamtmp", 2)], [("lamr", 1)])
        ACT(lambda h: h.activation(out=lamr[:, 2:4], in_=lamr[:, 0:2], func=AF.Exp), [("lamr", 0), ("lamr", 1)], [("lamr", 2)])
        DVE(lambda h: h.tensor_tensor(out=neglam, in0=lamr[:, 3:4], in1=lamr[:, 2:3], op=ALU.subtract),
            [("lamr", 2)], ["neglam"])
        DVE(lambda h: h.tensor_scalar(out=neglam, in0=neglam, scalar1=-LAM_INIT, scalar2=0.0, op0=ALU.add, op1=ALU.add),
            ["neglam"], ["neglam"])

    ident_in = din("ident_in", [128, 128])
    gains_r = [din("gain%d" % g, [128, NCH]) for g in range(4)]

    def init_tbx():
        ar = phase_arena()
        ext = ar.f32(768)
        DMA_S(lambda h: h.dma_start(out=ext[0:8, 0:257], in_=relb), [], ["ext"], "cst")
        S.barrier()
        DVE(lambda h: h.memset(ext[0:8, 257:768], 0.0), [], ["ext2"])
        DVE(lambda h: h.tensor_scalar(out=ext[0:8, 257:768], in0=ext[0:8, 257:768], scalar1=ext[0:8, 256:257], scalar2=0.0, op0=ALU.add, op1=ALU.add), ["ext", "ext2"], ["ext"])
        for kk in range(128):
            DMA_S(lambda h, kk=kk: h.dma_start(out=tbx[:, kk, 0:768], in_=ext[0:8, :]), ["ext"], ["tbx"], "tbx")

    class Pass:
        def __init__(self, kind, idx=0):
            self.kind, self.idx = kind, idx
            if kind == "sample":
                self.subs = [(0, 256)]
                self.tgs = [(64 * b, 64) for b in range(4)]
                self.W = 256
            else:
                self.subs = [(0, 512), (512, 512)]
                self.tgs = [(128 * g, 128) for g in range(8)]
                self.W = 1024
            self.slot = idx if kind == "prefix" else 3

        def si_of(self, c0):
            for si, (a, n) in enumerate(self.subs):
                if a <= c0 < a + n:
                    return si
            raise AssertionError(c0)

    WMODE = {}

    def load_w(dram2d, slot_ap, key, kparts, scr=None, defer_save=False, wname=None):
        mode = WMODE.get(wname, "cast") if scr is not None else "cast"
        if mode == "scr":
            DMA_S(lambda h: h.dma_start(out=slot_ap, in_=scr.rearrange("p (k n) -> p k n", k=kparts)), [], [key],
                  "w_%s_%d" % key)
            return
        src = dram2d.rearrange("(k p) n -> p k n", p=128)
        DMA_P(lambda h: h.dma_start(out=slot_ap, in_=src), [], [key], "w_%s_%d" % key)
        if mode == "save":
            def do_save():
                DMA_S(lambda h: h.dma_start(out=scr.rearrange("p (k n) -> p k n", k=kparts), in_=slot_ap), [key], [],
                      "ws_%s_%d" % key)
            if defer_save:
                return do_save
            do_save()
        return None

    def rms_norm_stats(ar, rstd_out, rstd_key, subs, rs=None):
        sq = [ar.bf(512) for _ in range(2)]
        if rs is None:
            rs = ar.f32(512)
        for si, (c0, n) in enumerate(subs):
            bk = next_bank(list(range(8)))
            for c in range(NCH):
                s = sq[c % 2]
                ACT(lambda h, s=s, c=c, c0=c0, n=n: h.activation(out=s[:, 0:n], in_=x[:, c, c0:c0 + n], func=AF.Square),
                    [("x", c, si)], [("sq", c % 2)])
                PE(lambda h, s=s, c=c, bk=bk, n=n: h.matmul(banks[bk][:, 0:n], lhsT=onesB, rhs=s[:, 0:n],
                                                          start=(c == 0), stop=(c == NCH - 1)),
                   [("sq", c % 2), "onesB"], [("bank", bk)])
            ACT(lambda h, bk=bk, n=n: h.activation(out=rs[:, 0:n], in_=banks[bk][:, 0:n], func=AF.Sqrt,
                                                   bias=epsT, scale=1.0 / D),
                [("bank", bk), "epsT"], ["rs"])
            DVE(lambda h, c0=c0, n=n: h.reciprocal(out=rstd_out[:, c0:c0 + n], in_=rs[:, 0:n]), ["rs"], [(rstd_key, si)])

    def apply_norm(hbuf, rstd_ap, rstd_key, gidx, subs):
        for si, (c0, n) in enumerate(subs):
            for c in range(NCH):
                DVE(lambda h, c=c, c0=c0, n=n: h.scalar_tensor_tensor(
                    out=hbuf[:, c, c0:c0 + n], in0=x[:, c, c0:c0 + n], scalar=gsb[:, gidx, c:c + 1],
                    in1=rstd_ap[:, c0:c0 + n], op0=ALU.mult, op1=ALU.mult),
                    [("x", c, si), (rstd_key, si), ("gsb", gidx)], [("h", c, si)])

    def load_x(P):
        ar = phase_arena()
        stg = [ar.f32(D) for _ in range(2)]
        for g, (c0, rows) in enumerate(P.tgs):
            s = stg[g % 2]
            if P.kind == "prefix":
                src = xpre[P.idx, c0:c0 + rows, :]
            elif P.kind == "own":
                src = xown[c0:c0 + rows, :]
            else:
                src = xs[g]
            DMA_S(lambda h, s=s, src=src, rows=rows: h.dma_start(out=s[0:rows, :], in_=src), [], [("xstg", g % 2)],
                  "xin%d" % (g % 2))
            si = P.si_of(c0)
            for q in range(4):
                bk = next_bank(list(range(8)))
                for j in range(4):
                    c = q * 4 + j
                    PE(lambda h, s=s, c=c, bk=bk, j=j, rows=rows: h.transpose(
                        out=banks[bk][:, j * 128: j * 128 + rows], in_=s[0:rows, c * 128:(c + 1) * 128],
                        identity=identF[0:rows, 0:rows]),
                       [("xstg", g % 2), "identF"], [("bank", bk)])
                outv = x[:, q * 4:(q + 1) * 4, c0:c0 + rows]
                inv = banks[bk][:].rearrange("p (j t) -> p j t", j=4)[:, :, 0:rows]
                wkeys = [("x", q * 4 + j, si) for j in range(4)]
                if q % 2 == 0:
                    ACT(lambda h, outv=outv, inv=inv: h.copy(out=outv, in_=inv), [("bank", bk)], wkeys)
                else:
                    DVE(lambda h, outv=outv, inv=inv: h.tensor_copy(out=outv, in_=inv), [("bank", bk)], wkeys)

    def ffn(wg, wu, wd, gidx, subs, sn):
        ar = phase_arena()
        hbuf = ar.bf(NCH * TT).rearrange("p (c t) -> p c t", c=NCH)
        rstd = ar.f32(TT)
        rms_norm_stats(ar, rstd, "rstd", subs)
        apply_norm(hbuf, rstd, "rstd", gidx, subs)
        NG = FF // 256
        NS = 2
        ND = 3
        wgs = [ar.bf(NCH * 256).rearrange("p (k n) -> p k n", k=NCH) for _ in range(NS)]
        wus = [ar.bf(NCH * 256).rearrange("p (k n) -> p k n", k=NCH) for _ in range(NS)]
        wds = [ar.bf(2 * D).rearrange("p (k n) -> p k n", k=2) for _ in range(ND)]
        hid = [ar.bf(2 * TT).rearrange("p (k t) -> p k t", k=2) for _ in range(ND)]
        sg = [ar.f32(512) for _ in range(2)]
        gu_banks = [0, 1, 2, 3]
        d_banks = [4, 5, 6, 7]
        cnt = [0, 0]

        def gu(g):
            sl = g % NS
            ds_ = g % ND
            load_w(wg[:, g * 256:(g + 1) * 256], wgs[sl], ("wg", sl), NCH, wscr[sn + "g"][g], wname=sn)
            load_w(wu[:, g * 256:(g + 1) * 256], wus[sl], ("wu", sl), NCH, wscr[sn + "u"][g], wname=sn)
            load_w(wd[g * 256:(g + 1) * 256, :], wds[ds_], ("wd", ds_), 2, wscr[sn + "d"][g], wname=sn)
            hs = g % ND
            for cc in range(2):
                for si, (c0, n) in enumerate(subs):
                    u = cnt[0]; cnt[0] += 1
                    bg = gu_banks[(2 * u) % 4]; bu = gu_banks[(2 * u + 1) % 4]
                    for k in range(NCH):
                        PE(lambda h, k=k, bg=bg, sl=sl, cc=cc, c0=c0, n=n: h.matmul(
                            banks[bg][:, 0:n], lhsT=wgs[sl][:, k, cc * 128:(cc + 1) * 128], rhs=hbuf[:, k, c0:c0 + n],
                            start=(k == 0), stop=(k == NCH - 1)),
                           [("wg", sl), ("h", k, si)], [("bank", bg)])
                    for k in range(NCH):
                        PE(lambda h, k=k, bu=bu, sl=sl, cc=cc, c0=c0, n=n: h.matmul(
                            banks[bu][:, 0:n], lhsT=wus[sl][:, k, cc * 128:(cc + 1) * 128], rhs=hbuf[:, k, c0:c0 + n],
                            start=(k == 0), stop=(k == NCH - 1)),
                           [("wu", sl), ("h", k, si)], [("bank", bu)])
                    s = sg[u % 2]
                    ACT(lambda h, s=s, bg=bg, n=n: h.activation(out=s[:, 0:n], in_=banks[bg][:, 0:n], func=AF.Silu),
                        [("bank", bg)], [("sg", u % 2)])
                    DVE(lambda h, s=s, bu=bu, hs=hs, cc=cc, c0=c0, n=n: h.tensor_tensor(
                        out=hid[hs][:, cc, c0:c0 + n], in0=s[:, 0:n], in1=banks[bu][:, 0:n], op=ALU.mult),
                        [("sg", u % 2), ("bank", bu)], [("hid", hs, cc, si)])

        def down2(G):
            gs = (2 * G, 2 * G + 1)
            for m in range(NCH):
                for si, (c0, n) in enumerate(subs):
                    u = cnt[1]; cnt[1] += 1
                    bd = d_banks[u % 4]
                    idx = 0
                    for g in gs:
                        for cc in range(2):
                            PE(lambda h, cc=cc, bd=bd, g=g, m=m, c0=c0, n=n, idx=idx: h.matmul(
                                banks[bd][:, 0:n], lhsT=wds[g % ND][:, cc, m * 128:(m + 1) * 128],
                                rhs=hid[g % ND][:, cc, c0:c0 + n], start=(idx == 0), stop=(idx == 3)),
                               [("wd", g % ND), ("hid", g % ND, cc, si)], [("bank", bd)])
                            idx += 1
                    DVE(lambda h, bd=bd, m=m, c0=c0, n=n: h.scalar_tensor_tensor(
                        out=x[:, m, c0:c0 + n], in0=banks[bd][:, 0:n], scalar=0.5, in1=x[:, m, c0:c0 + n],
                        op0=ALU.mult, op1=ALU.add),
                        [("bank", bd), ("x", m, si)], [("x", m, si)])

        gu(0)
        gu(1)
        for G in range(NG // 2):
            if 2 * G + 2 < NG:
                gu(2 * G + 2)
            down2(G)
            if 2 * G + 3 < NG:
                gu(2 * G + 3)

    def proj_qkv(P):
        ar = phase_arena(True)
        W = P.W
        hbuf = ar.bf(NCH * TT).rearrange("p (c t) -> p c t", c=NCH)
        t1 = ar.f32(TT)
        rms_norm_stats(ar, rstd_mix, "rstd_mix", P.subs, rs=t1[:, 0:512])
        apply_norm(hbuf, rstd_mix, "rstd_mix", 1, P.subs)
        cosb = ar.f32(TT); sinb = ar.f32(TT)
        if P.kind == "prefix":
            csrc, ssrc = cosP[P.idx], sinP[P.idx]
        elif P.kind == "own":
            csrc, ssrc = cosO, sinO
        else:
            csrc, ssrc = cosS, sinS
        DMA_S(lambda h: h.dma_start(out=cosb[:, 0:W], in_=csrc), [], ["cos"], "cs_c")
        DMA_S(lambda h: h.dma_start(out=sinb[:, 0:W], in_=ssrc), [], ["sin"], "cs_s")
        NS = 2
        wsl = [ar.bf(NCH * 256).rearrange("p (k n) -> p k n", k=NCH) for _ in range(NS)]
        zf = [ar.f32(TT) for _ in range(2)]
        zs1 = ar.f32(TT)
        kst = [ar.f32(512).rearrange("p (g f) -> p g f", g=4) for _ in range(2)]
        vst = [ar.f32(256) for _ in range(3)]
        gcount = [0]
        ucount = [0]
        kcount = [0]
        vcount = [0]

        pending = []

        def flush_pending():
            while pending:
                pending.pop(0)()

        def k_outputs(src, src_key, hd, is_a):
            if P.kind != "sample":
                ks = kaT_s if is_a else kbT_s
                DMA_P(lambda h: h.dma_start(out=ks[hd, :, P.slot * 1024:(P.slot + 1) * 1024], in_=src[:, 0:1024]),
                      [src_key], [("kscr", is_a, P.slot)], "kscr")
            else:
                kst_small = ksTa if is_a else ksTb
                DVE(lambda h: h.tensor_copy(out=kst_small[:, hd, :], in_=src[:, 0:256]), [src_key], [("kst", is_a, hd)])
            if P.kind == "prefix":
                return
            pending.append(lambda: k_out_rows(src, src_key, hd, is_a))

        def k_out_rows(src, src_key, hd, is_a):
            if P.kind == "own":
                groups = list(range(8)) if is_a else [4, 5, 6, 7]
            else:
                groups = [0, 1, 2, 3]
            for b0 in range(0, len(groups), 4):
                gl = groups[b0:b0 + 4]
                bk = next_bank(list(range(8)))
                for j, g in enumerate(gl):
                    c0, rows = P.tgs[g]
                    PE(lambda h, j=j, c0=c0, rows=rows, bk=bk: h.transpose(
                        out=banks[bk][0:rows, j * 128:(j + 1) * 128], in_=src[:, c0:c0 + rows], identity=identF),
                       [src_key, "identF"], [("bank", bk)])
                kc_ = kcount[0]; kcount[0] += 1
                st = kst[kc_ % 2]
                ACT(lambda h, st=st, bk=bk: h.copy(out=st, in_=banks[bk][:].rearrange("p (g f) -> p g f", g=4)),
                    [("bank", bk)], [("kst_stage", kc_ % 2)])
                for j, g in enumerate(gl):
                    c0, rows = P.tgs[g]
                    hs_ = slice(hd * 128, (hd + 1) * 128)
                    if P.kind == "own":
                        dst = akp[c0:c0 + rows, hs_] if is_a else bkp[c0 - 512:c0 - 512 + rows, hs_]
                    else:
                        dst = aks[g, :, hs_] if is_a else bks[g, 448:512, hs_]
                    DMA_A(lambda h, st=st, j=j, rows=rows, dst=dst: h.dma_start(out=dst, in_=st[0:rows, j, :]),
                          [("kst_stage", kc_ % 2)], [], "kout%d" % (kc_ % 2))
                    out_streams.add("kout%d" % (kc_ % 2))

        def feat_group(colbase, kind, hd0, sl):
            for cc in range(2):
                hd = hd0 + cc
                u = ucount[0]; ucount[0] += 1
                z = zf[u % 2]
                for si, (c0, n) in enumerate(P.subs):
                    bk = next_bank(list(range(8)))
                    for k in range(NCH):
                        PE(lambda h, k=k, bk=bk, cc=cc, c0=c0, n=n: h.matmul(
                            banks[bk][:, 0:n], lhsT=wsl[sl][:, k, cc * 128:(cc + 1) * 128], rhs=hbuf[:, k, c0:c0 + n],
                            start=(k == 0), stop=(k == NCH - 1)),
                           [("wi", sl), ("h", k, si)], [("bank", bk)])
                    if kind == "qb":
                        ACT(lambda h, bk=bk, hd=hd, c0=c0, n=n: h.copy(out=qb[:, hd, c0:c0 + n], in_=banks[bk][:, 0:n]),
                            [("bank", bk)], [("qb", hd, si)])
                    else:
                        ACT(lambda h, bk=bk, z=z, c0=c0, n=n: h.copy(out=z[:, c0:c0 + n], in_=banks[bk][:, 0:n]),
                            [("bank", bk)], [("zf", u % 2)])
                flush_pending()
                if kind == "qb":
                    continue
                if kind == "kb":
                    k_outputs(z, ("zf", u % 2), hd, False)
                    continue
                zz = zs1
                for blk in range(4):
                    p0 = blk * 32
                    p1 = p0 ^ 32
                    DMA_S(lambda h, z=z, zz=zz, p0=p0, p1=p1: h.dma_start(out=zz[p0:p0 + 32, 0:W], in_=z[p1:p1 + 32, 0:W]),
                          [("zf", u % 2)], ["zs"], "swap")
                DVE(lambda h, z=z: h.tensor_tensor(out=t1[:, 0:W], in0=z[:, 0:W], in1=cosb[:, 0:W], op=ALU.mult),
                    [("zf", u % 2), "cos"], ["t1"])
                DVE(lambda h, zz=zz: h.tensor_tensor(out=zz[:, 0:W], in0=zz[:, 0:W], in1=sinb[:, 0:W], op=ALU.mult),
                    ["zs", "sin"], ["zs"])
                if kind == "qa":
                    DVE(lambda h, zz=zz, hd=hd: h.tensor_tensor(out=qa[:, hd, 0:W], in0=t1[:, 0:W], in1=zz[:, 0:W], op=ALU.add),
                        ["t1", "zs"], [("qa", hd, si_) for si_ in range(len(P.subs))])
                else:
                    DVE(lambda h, zz=zz, z=z: h.tensor_tensor(out=z[:, 0:W], in0=t1[:, 0:W], in1=zz[:, 0:W], op=ALU.add),
                        ["t1", "zs"], [("zf", u % 2)])
                    k_outputs(z, ("zf", u % 2), hd, True)

        def tok_group(colbase, is_a, hd0, sl):
            vs = va_s if is_a else vb_s
            cs = slice(hd0 * 128, hd0 * 128 + 256)
            for g, (c0, rows) in enumerate(P.tgs):
                bk = next_bank(list(range(8)))
                for k in range(NCH):
                    PE(lambda h, k=k, bk=bk, c0=c0, rows=rows: h.matmul(
                        banks[bk][0:rows, 0:256], lhsT=hbuf[:, k, c0:c0 + rows], rhs=wsl[sl][:, k, :],
                        start=(k == 0), stop=(k == NCH - 1)),
                       [("wi", sl), ("h", k, P.si_of(c0))], [("bank", bk)])
                vc_ = vcount[0]; vcount[0] += 1
                st = vst[vc_ % 3]
                ACT(lambda h, st=st, bk=bk, rows=rows: h.copy(out=st[0:rows, :], in_=banks[bk][0:rows, 0:256]),
                    [("bank", bk)], [("vst", vc_ % 3)])
                if g == 0:
                    flush_pending()
                dsts = []
                if P.kind != "sample":
                    kt = P.slot * 8 + g
                    DMA_P(lambda h, st=st, kt=kt: h.dma_start(
                        out=vs[hd0:hd0 + 2, :, kt, :].rearrange("h p d -> p h d"),
                        in_=st.rearrange("p (h d) -> p h d", h=2)),
                        [("vst", vc_ % 3)], [("vscr", is_a, P.slot)], "vscr")
                    if P.kind == "own":
                        if is_a:
                            dsts = [avp[c0:c0 + rows, cs]]
                        elif g >= 4:
                            dsts = [bvp[c0 - 512:c0 - 512 + rows, cs]]
                else:
                    dsts = [avs[g, :, cs] if is_a else bvs[g, 448:512, cs]]
                    vsm = vsa if is_a else vsb
                    DVE(lambda h, st=st, vsm=vsm, g=g: h.tensor_copy(out=vsm[0:64, g, cs], in_=st[0:64, :]),
                        [("vst", vc_ % 3)], [("vsm", is_a, hd0, g)])
                for dst in dsts:
                    DMA_A(lambda h, st=st, rows=rows, dst=dst: h.dma_start(out=dst, in_=st[0:rows, :]),
                          [("vst", vc_ % 3)], [], "vout%d" % (vc_ % 3))
                    out_streams.add("vout%d" % (vc_ % 3))

        ka_g = [("f", C_KA + 256 * j, "ka", 2 * j) for j in range(4)]
        va_g = [("t", C_VA + 256 * j, True, 2 * j) for j in range(4)]
        kb_g = [("f", C_KB + 256 * j, "kb", 2 * j) for j in range(4)]
        vb_g = [("t", C_VB + 256 * j, False, 2 * j) for j in range(4)]
        qa_g = [("f", C_QA + 256 * j, "qa", 2 * j) for j in range(4)]
        qb_g = [("f", C_QB + 256 * j, "qb", 2 * j) for j in range(4)]
        glist = []
        if P.kind != "prefix":
            for j in range(4):
                glist += [qa_g[j], qb_g[j]]
        for j in range(4):
            glist += [ka_g[j], va_g[j]]
        for j in range(4):
            glist += [kb_g[j], vb_g[j]]

        saves = {}

        def issue_load(i):
            colbase = glist[i][1]
            wn = "wi_q" if glist[i][2] in ("qa", "qb") else "wi_kv"
            saves[i] = load_w(win[:, colbase:colbase + 256], wsl[i % NS], ("wi", i % NS), NCH,
                              wscr["wi"][colbase // 256], defer_save=True, wname=wn)

        issue_load(0)
        for i, (typ, colbase, a3, hd0) in enumerate(glist):
            if i + 1 < len(glist):
                issue_load(i + 1)
            if typ == "f":
                feat_group(colbase, a3, hd0, i % NS)
            else:
                tok_group(colbase, a3, hd0, i % NS)
            if saves.get(i) is not None:
                saves[i]()
        flush_pending()
        if P.kind == "sample":
            for b in range(4):
                DMA_S(lambda h, b=b: h.dma_start(out=bks[b, 0:448, :], in_=cbk[b, 64:512, :]), [], [], "roll")
                DMA_S(lambda h, b=b: h.dma_start(out=bvs[b, 0:448, :], in_=cbv[b, 64:512, :]), [], [], "roll")
            out_streams.add("roll")

    def attention(P):
        ar = phase_arena(True)
        kT = [ar.bf(4096) for _ in range(2)]
        vv = [ar.bf(32 * 128).rearrange("p (k d) -> p k d", k=32) for _ in range(2)]
        Pt = [ar.bf(512) for _ in range(6)] if P.kind == "own" else None
        ef = [ar.f32(512) for _ in range(4)]
        tb = ar.f32(8 * 640).rearrange("p (h q) -> p h q", h=8)
        src = bass.AP(tbx_t, 128, [[1023, 128], [128 * 1024, 8], [1, 640]])
        DMA_S(lambda h: h.dma_start(out=tb, in_=src), ["tbx"], ["tb"], "tbl")
        pcnt = [0]
        ecnt = [0]
        SCL_A = 64 ** -0.5
        SCL_B = 128 ** -0.5
        allk = lambda is_a: [("kscr", is_a, t) for t in range(4)]
        allv = lambda is_a: [("vscr", is_a, t) for t in range(4)]

        def epilogue_a(Ob, Db, width, out_ap, out_keys):
            e0, e1, e2, e3 = [ef[(ecnt[0] + j) % 4] for j in range(4)]
            k0, k1, k2, k3 = [("ef", (ecnt[0] + j) % 4) for j in range(4)]
            ecnt[0] += 4
            DVE(lambda h: h.reciprocal(out=e0[:, 0:width], in_=banks[Db[0]][:, 0:width]), [("bank", Db[0])], [k0])
            DVE(lambda h: h.reciprocal(out=e1[:, 0:width], in_=banks[Db[1]][:, 0:width]), [("bank", Db[1])], [k1])
            DVE(lambda h: h.tensor_tensor(out=e0[:, 0:width], in0=e0[:, 0:width], in1=banks[Ob[0]][:, 0:width], op=ALU.mult),
                [k0, ("bank", Ob[0])], [k0])
            DVE(lambda h: h.tensor_tensor(out=e1[:, 0:width], in0=e1[:, 0:width], in1=banks[Ob[1]][:, 0:width], op=ALU.mult),
                [k1, ("bank", Ob[1])], [k1])
            DVE(lambda h: h.scalar_tensor_tensor(out=e2[:, 0:width], in0=e1[:, 0:width], scalar=neglam, in1=e0[:, 0:width],
                                                 op0=ALU.mult, op1=ALU.add), [k0, k1, "neglam"], [k2])
            ACT(lambda h: h.activation(out=e3[:, 0:width], in_=e2[:, 0:width], func=AF.Square), [k2], [k3])
            bk = Db[0]
            PE(lambda h: h.matmul(banks[bk][:, 0:width], lhsT=onesF, rhs=e3[:, 0:width], start=True, stop=True),
               [k3, "onesF"], [("bank", bk)])
            ACT(lambda h: h.activation(out=e0[:, 0:width], in_=banks[bk][:, 0:width], func=AF.Sqrt, bias=epsT,
                                       scale=1.0 / 128), [("bank", bk), "epsT"], [k0])
            DVE(lambda h: h.reciprocal(out=e1[:, 0:width], in_=e0[:, 0:width]), [k0], [k1])
            DVE(lambda h: h.scalar_tensor_tensor(out=out_ap, in0=e2[:, 0:width], scalar=gsub, in1=e1[:, 0:width],
                                                 op0=ALU.mult, op1=ALU.mult), [k2, k1, "gsub"], out_keys)

        def prompt_a():
            for hd in range(8):
                b = hd % 2
                DMA_S(lambda h, hd=hd, b=b: h.dma_start(out=kT[b], in_=kaT_s[hd]), allk(True), [("kT", b)], "kT%d" % b)
                DMA_S(lambda h, hd=hd, b=b: h.dma_start(out=vv[b], in_=va_s[hd]), allv(True), [("vv", b)], "vv%d" % b)
                for qt in range(2):
                    q0 = 512 * qt
                    nkt = 24 + 4 * qt + 4
                    O = (4, 5); Dn = (6, 7)
                    def front(kt, hd=hd, b=b, qt=qt, q0=q0):
                        j = kt - (24 + 4 * qt)
                        c0 = 128 * j if j >= 0 else 0
                        sb = (0, 1) if kt % 2 == 0 else (2, 3)
                        for m in range(2):
                            pi = (kt % 3) * 2 + m
                            Pm = Pt[pi]
                            ps = slice(64 * m, 64 * m + 64)
                            PE(lambda h, ps=ps, kt=kt, c0=c0, sbm=sb[m]: h.matmul(
                                banks[sbm][:, c0:512], lhsT=kT[b][ps, kt * 128:(kt + 1) * 128],
                                rhs=qa[ps, hd, q0 + c0:q0 + 512], start=True, stop=True),
                               [("kT", b), ("qa", hd, qt)], [("bank", sb[m])])
                            if kt < 24:
                                sidx = kt // 8
                                ACT(lambda h, Pm=Pm, sbm=sb[m], sidx=sidx: h.activation(
                                    out=Pm[:, 0:512], in_=banks[sbm][:, 0:512], func=AF.Exp, scale=SCL_A,
                                    bias=kbias[:, sidx:sidx + 1]), [("bank", sb[m]), "kbias"], [("P", pi)])
                            else:
                                ACT(lambda h, Pm=Pm, sbm=sb[m], c0=c0: h.activation(
                                    out=Pm[:, c0:512], in_=banks[sbm][:, c0:512], func=AF.Exp, scale=SCL_A),
                                    [("bank", sb[m])], [("P", pi)])
                            if j >= 0:
                                DVE(lambda h, Pm=Pm, c0=c0: h.memset(Pm[64:128, c0:c0 + 64], 0.0), [("P", pi)], [("P", pi)])

                    def back(kt, b=b, qt=qt, nkt=nkt):
                        j = kt - (24 + 4 * qt)
                        c0 = 128 * j if j >= 0 else 0
                        for m in range(2):
                            pi = (kt % 3) * 2 + m
                            Pm = Pt[pi]
                            PE(lambda h, Pm=Pm, m=m, kt=kt, c0=c0: h.matmul(
                                banks[O[m]][:, c0:512], lhsT=vv[b][:, kt, :], rhs=Pm[:, c0:512],
                                start=(kt == 0), stop=(kt == nkt - 1), skip_group_check=True),
                               [("vv", b), ("P", pi)], [("bank", O[m])])
                            PE(lambda h, Pm=Pm, m=m, kt=kt, c0=c0: h.matmul(
                                banks[Dn[m]][:, c0:512], lhsT=onesB, rhs=Pm[:, c0:512],
                                start=(kt == 0), stop=(kt == nkt - 1), skip_group_check=True),
                               [("P", pi), "onesB"], [("bank", Dn[m])])

                    for t in range(nkt + 1):
                        if t < nkt:
                            front(t)
                        if t >= 1:
                            back(t - 1)
                    epilogue_a(O, Dn, 512, qa[:, hd, q0:q0 + 512], [("qa", hd, qt)])

        def prompt_b():
            for hd in range(8):
                b = hd % 2
                DMA_S(lambda h, hd=hd, b=b: h.dma_start(out=kT[b], in_=kbT_s[hd]), allk(False), [("kT", b)], "kT%d" % b)
                DMA_S(lambda h, hd=hd, b=b: h.dma_start(out=vv[b], in_=vb_s[hd]), allv(False), [("vv", b)], "vv%d" % b)
                for qt in range(2):
                    units = []
                    if qt == 0:
                        for s_ in range(3):
                            for t in range(4):
                                units.append((1024 * s_ + 512 + 128 * t, -512 + 128 * t, 0, 128 * t + 128, 3 + s_))
                    for kt in range(8):
                        kstart = 128 * kt
                        a = max(kstart, 512 * qt)
                        e_ = min(kstart + 640, 512 * qt + 512, 1024)
                        if e_ > a:
                            units.append((3072 + 128 * kt, kstart, a, e_, None))
                    OBK, DBK = 4, 5
                    nu = len(units)
                    def frontb(ui, hd=hd, b=b, qt=qt, units=units):
                        kcol, kstart, a, e_, bcol = units[ui]
                        n = e_ - a
                        a_ = a - 512 * qt
                        sbk = ui % 4
                        pi = ui % 6
                        Pm = Pt[pi]
                        ei = ui % 4
                        e = ef[ei]
                        PE(lambda h: h.matmul(
                            banks[sbk][:, a_:a_ + n], lhsT=kT[b][:, kcol:kcol + 128], rhs=qb[:, hd, a:a + n],
                            start=True, stop=True),
                           [("kT", b), ("qb", hd, qt)], [("bank", sbk)])
                        qq0 = a - kstart
                        DVE(lambda h: h.scalar_tensor_tensor(
                            out=e[:, a_:a_ + n], in0=banks[sbk][:, a_:a_ + n], scalar=SCL_B, in1=tb[:, hd, qq0:qq0 + n],
                            op0=ALU.mult, op1=ALU.add), [("bank", sbk), "tb"], [("ef", ei)])
                        if bcol is None:
                            ACT(lambda h: h.activation(out=Pm[:, a_:a_ + n], in_=e[:, a_:a_ + n], func=AF.Exp),
                                [("ef", ei)], [("P", pi)])
                        else:
                            ACT(lambda h: h.activation(
                                out=Pm[:, a_:a_ + n], in_=e[:, a_:a_ + n], func=AF.Exp, bias=kbias[:, bcol:bcol + 1]),
                                [("ef", ei), "kbias"], [("P", pi)])
                        lo, hi_ = max(kstart, a), min(kstart + 64, e_)
                        if hi_ > lo:
                            DVE(lambda h, lo=lo, hi_=hi_: h.memset(Pm[64:128, lo - 512 * qt:hi_ - 512 * qt], 0.0),
                                [("P", pi)], [("P", pi)])
                        lo, hi_ = max(kstart + 576, a), min(kstart + 640, e_)
                        if hi_ > lo:
                            DVE(lambda h, lo=lo, hi_=hi_: h.memset(Pm[0:64, lo - 512 * qt:hi_ - 512 * qt], 0.0),
                                [("P", pi)], [("P", pi)])

                    def backb(ui, b=b, qt=qt, units=units, nu=nu):
                        kcol, kstart, a, e_, bcol = units[ui]
                        n = e_ - a
                        a_ = a - 512 * qt
                        pi = ui % 6
                        Pm = Pt[pi]
                        PE(lambda h: h.matmul(
                            banks[OBK][:, a_:a_ + n], lhsT=vv[b][:, kcol // 128, :], rhs=Pm[:, a_:a_ + n],
                            start=(ui == 0), stop=(ui == nu - 1), skip_group_check=True),
                           [("vv", b), ("P", pi)], [("bank", OBK)])
                        PE(lambda h: h.matmul(
                            banks[DBK][:, a_:a_ + n], lhsT=onesB, rhs=Pm[:, a_:a_ + n],
                            start=(ui == 0), stop=(ui == nu - 1), skip_group_check=True),
                           [("P", pi), "onesB"], [("bank", DBK)])

                    for t in range(nu + 2):
                        if t < nu:
                            frontb(t)
                        if t >= 2:
                            backb(t - 2)
                    ei = ecnt[0] % 4; ecnt[0] += 1
                    e = ef[ei]
                    DVE(lambda h, e=e: h.reciprocal(out=e, in_=banks[DBK][:, :]), [("bank", DBK)], [("ef", ei)])
                    DVE(lambda h, e=e, hd=hd, qt=qt: h.tensor_tensor(out=qb[:, hd, 512 * qt:512 * qt + 512], in0=e,
                                                                    in1=banks[OBK][:, :], op=ALU.mult),
                        [("ef", ei), ("bank", OBK)], [("qb", hd, qt)])

        def sample_all():
            kc = [t_.rearrange("p (k f) -> p k f", k=4) for t_ in kT]
            vc = [t_.rearrange("p k d -> p (k d)").rearrange("p (k f) -> p k f", k=4) for t_ in vv]
            kTs = [ar.bf(1024).rearrange("p (h k) -> p h k", h=8) for _ in range(2)]
            Ps = [ar.bf(1024) for _ in range(2)]
            qbd = ar.bf(1024).rearrange("p (h c) -> p h c", h=8)
            sa = [ar.f32(512) for _ in range(2)]
            oh = ar.f32(512)
            DVE(lambda h: h.memset(qbd, 0.0), [], ["qbd"])
            TRB, SB0, SB1, OB0, OB1, DB0, DB1 = 0, 1, 2, 3, 4, 5, 6
            qk = [("qa", hd, 0) for hd in range(8)]
            qkb = [("qb", hd, 0) for hd in range(8)]
            cnt_s = [0]

            def sample_tiles(bi, is_a, ckey, cval, nkt_cache):
                qc = slice(64 * bi, 64 * bi + 64)
                nk = nkt_cache + 1
                slot_of = {}
                for kt in range(nkt_cache):
                    if kt % 4 == 0:
                        cur = cnt_s[0] % 2; cnt_s[0] += 1
                    slot_of[kt] = cur
                slot_of[nkt_cache] = cur

                def T(kt):
                    if kt == nkt_cache:
                        return
                    sl = slot_of[kt]
                    if kt % 4 == 0:
                        DMA_P(lambda h: h.dma_start(
                            out=kc[sl], in_=ckey[bi, kt * 128:(kt + 4) * 128, :].rearrange("(k p) f -> p k f", p=128)),
                            [], [("kT", sl)], "kc%d" % sl)
                        DMA_P(lambda h: h.dma_start(
                            out=vc[sl], in_=cval[bi, kt * 128:(kt + 4) * 128, :].rearrange("(k p) f -> p k f", p=128)),
                            [], [("vv", sl)], "vc%d" % sl)
                    ks = kt % 2
                    trb = banks[TRB][:].bitcast(BF16).rearrange("p (h k) -> p h k", h=8)
                    for hd in range(8):
                        PE(lambda h, hd=hd: h.transpose(
                            out=trb[:, hd, :], in_=kc[sl][:, kt % 4, hd * 128:(hd + 1) * 128], identity=identB),
                           [("kT", sl), "identB"], [("bank", TRB)])
                    DVE(lambda h: h.tensor_copy(out=kTs[ks], in_=trb), [("bank", TRB)], [("kTs", ks)])

                def Fr(kt):
                    new = (kt == nkt_cache)
                    kp = 64 if new else 128
                    ks = kt % 2
                    ps_ = Ps[ks]
                    if is_a:
                        for hd in range(8):
                            sbk = SB0 if hd < 4 else SB1
                            lhs = (ksTa[:, hd, qc] if new else kTs[ks][:, hd, :])
                            PE(lambda h, hd=hd, sbk=sbk, lhs=lhs: h.matmul(
                                banks[sbk][0:kp, (hd % 4) * 128:(hd % 4 + 1) * 128], lhsT=lhs, rhs=qbd[:, hd, :],
                                start=True, stop=True, skip_group_check=True),
                               [("kTs", ks), "qbd", ("kst", True, hd)], [("bank", sbk)])
                        for half, sbk in ((0, SB0), (1, SB1)):
                            ACT(lambda h, half=half, sbk=sbk: h.activation(
                                out=ps_[0:kp, half * 512:(half + 1) * 512], in_=banks[sbk][0:kp, :], func=AF.Exp, scale=SCL_A),
                                [("bank", sbk)], [("Ps", ks, half)])
                    else:
                        qq0 = 0 if new else 512 - 128 * kt
                        for hd in range(8):
                            lhs = (ksTb[:, hd, qc] if new else kTs[ks][:, hd, :])
                            PE(lambda h, hd=hd, lhs=lhs: h.matmul(
                                banks[SB0][0:kp, hd * 64:(hd + 1) * 64], lhsT=lhs, rhs=qb[:, hd, qc],
                                start=True, stop=True, skip_group_check=True),
                               [("kTs", ks), ("qb", hd, 0), ("kst", False, hd)], [("bank", SB0)])
                        ei = ecnt[0] % 4; ecnt[0] += 1
                        e = ef[ei]
                        DVE(lambda h: h.scalar_tensor_tensor(
                            out=e[0:kp, :].rearrange("p (h q) -> p h q", h=8),
                            in0=banks[SB0][0:kp, :].rearrange("p (h q) -> p h q", h=8), scalar=SCL_B,
                            in1=tb[0:kp, :, qq0:qq0 + 64], op0=ALU.mult, op1=ALU.add),
                            [("bank", SB0), "tb"], [("ef", ei)])
                        ACT(lambda h: h.activation(out=ps_[0:kp, 0:512], in_=e[0:kp, :], func=AF.Exp),
                            [("ef", ei)], [("Ps", ks, 0)])

                def Bk(kt):
                    new = (kt == nkt_cache)
                    kp = 64 if new else 128
                    ks = kt % 2
                    sl = slot_of[kt]
                    ps_ = Ps[ks]
                    if is_a:
                        for hd in range(8):
                            obk = OB0 if hd < 4 else OB1
                            lhs = (vsa[0:64, bi, hd * 128:(hd + 1) * 128] if new else vc[sl][:, kt % 4, hd * 128:(hd + 1) * 128])
                            PE(lambda h, hd=hd, obk=obk, lhs=lhs: h.matmul(
                                banks[obk][:, (hd % 4) * 128:(hd % 4 + 1) * 128], lhsT=lhs,
                                rhs=ps_[0:kp, hd * 128:(hd + 1) * 128],
                                start=(kt == 0 and hd % 4 == 0), stop=(kt == nkt_cache), skip_group_check=True),
                               [("vv", sl), ("Ps", ks, hd // 4), ("vsm", True, 2 * (hd // 2), bi)], [("bank", obk)])
                        for half, dbk in ((0, DB0), (1, DB1)):
                            PE(lambda h, half=half, dbk=dbk: h.matmul(
                                banks[dbk][:, :], lhsT=onesB[0:kp, :], rhs=ps_[0:kp, half * 512:(half + 1) * 512],
                                start=(kt == 0), stop=(kt == nkt_cache), skip_group_check=True),
                               [("Ps", ks, half), "onesB"], [("bank", dbk)])
                    else:
                        for hd in range(8):
                            lhs = (vsb[0:64, bi, hd * 128:(hd + 1) * 128] if new else vc[sl][:, kt % 4, hd * 128:(hd + 1) * 128])
                            PE(lambda h, hd=hd, lhs=lhs: h.matmul(
                                banks[OB0][:, hd * 64:(hd + 1) * 64], lhsT=lhs, rhs=ps_[0:kp, hd * 64:(hd + 1) * 64],
                                start=(kt == 0 and hd == 0), stop=(kt == nkt_cache), skip_group_check=True),
                               [("vv", sl), ("Ps", ks, 0), ("vsm", False, 2 * (hd // 2), bi)], [("bank", OB0)])
                        PE(lambda h: h.matmul(
                            banks[DB0][:, :], lhsT=onesB[0:kp, :], rhs=ps_[0:kp, 0:512],
                            start=(kt == 0), stop=(kt == nkt_cache), skip_group_check=True),
                           [("Ps", ks, 0), "onesB"], [("bank", DB0)])

                for t in range(nk + 2):
                    if t < nk:
                        T(t)
                    if 1 <= t <= nk:
                        Fr(t - 1)
                    if t >= 2:
                        Bk(t - 2)

            for bi in range(4):
                qc = slice(64 * bi, 64 * bi + 64)
                DVE(lambda h, qc=qc: h.tensor_copy(out=qbd[0:64, :, 0:64], in_=qa[0:64, :, qc]), qk + ["qbd"], ["qbd"])
                DVE(lambda h, qc=qc: h.tensor_copy(out=qbd[64:128, :, 64:128], in_=qa[64:128, :, qc]), qk + ["qbd"], ["qbd"])
                sample_tiles(bi, True, cak, cav, 16)
                for half, (obk, dbk) in enumerate(((OB0, DB0), (OB1, DB1))):
                    s_ = sa[half]
                    DVE(lambda h, s_=s_, dbk=dbk: h.reciprocal(out=s_, in_=banks[dbk][:, :]), [("bank", dbk)], [("sa", half)])
                    DVE(lambda h, s_=s_, obk=obk: h.tensor_tensor(out=s_, in0=s_, in1=banks[obk][:, :], op=ALU.mult),
                        [("sa", half), ("bank", obk)], [("sa", half)])
                    sv = s_.rearrange("p (h m q) -> p h m q", h=4, m=2)
                    DVE(lambda h, sv=sv, half=half: h.scalar_tensor_tensor(
                        out=oh[:, half * 256:(half + 1) * 256].rearrange("p (h q) -> p h q", h=4), in0=sv[:, :, 1, :],
                        scalar=neglam, in1=sv[:, :, 0, :], op0=ALU.mult, op1=ALU.add),
                        [("sa", half), "neglam"], [("oh", half)])
                e3 = ef[0]; e0 = ef[1]; e1 = ef[2]
                ACT(lambda h: h.activation(out=e3, in_=oh, func=AF.Square), [("oh", 0), ("oh", 1)], [("ef", 0)])
                PE(lambda h: h.matmul(banks[DB0][:, :], lhsT=onesF, rhs=e3, start=True, stop=True), [("ef", 0), "onesF"],
                   [("bank", DB0)])
                ACT(lambda h: h.activation(out=e0, in_=banks[DB0][:, :], func=AF.Sqrt, bias=epsT, scale=1.0 / 128),
                    [("bank", DB0), "epsT"], [("ef", 1)])
                DVE(lambda h: h.reciprocal(out=e1, in_=e0), [("ef", 1)], [("ef", 2)])
                DVE(lambda h, qc=qc: h.scalar_tensor_tensor(
                    out=qa[:, :, qc], in0=oh.rearrange("p (h q) -> p h q", h=8), scalar=gsub,
                    in1=e1.rearrange("p (h q) -> p h q", h=8), op0=ALU.mult, op1=ALU.mult),
                    [("oh", 0), ("oh", 1), ("ef", 2), "gsub", "qbd"], qk)
                sample_tiles(bi, False, cbk, cbv, 4)
                ei = ecnt[0] % 4; ecnt[0] += 1
                e = ef[ei]
                DVE(lambda h, e=e: h.reciprocal(out=e, in_=banks[DB0][:, :]), [("bank", DB0)], [("ef", ei)])
                DVE(lambda h, e=e, qc=qc: h.tensor_tensor(out=qb[:, :, qc], in0=e.rearrange("p (h q) -> p h q", h=8),
                                                         in1=banks[OB0][:, :].rearrange("p (h q) -> p h q", h=8), op=ALU.mult),
                    [("ef", ei), ("bank", OB0)], qkb)

        if P.kind == "own":
            prompt_a()
            S.barrier()
            prompt_b()
        else:
            sample_all()

    def merge(subs):
        ar = phase_arena(True)
        hbuf = ar.bf(NCH * TT).rearrange("p (c t) -> p c t", c=NCH)
        apply_norm(hbuf, rstd_mix, "rstd_mix", 1, subs)
        NS = 2
        wga = [ar.bf(NCH * 128).rearrange("p (k n) -> p k n", k=NCH) for _ in range(NS)]
        wgb = [ar.bf(NCH * 128).rearrange("p (k n) -> p k n", k=NCH) for _ in range(NS)]
        wa = [ar.bf(8 * 128).rearrange("p (k n) -> p k n", k=8) for _ in range(NS)]
        wb = [ar.bf(8 * 128).rearrange("p (k n) -> p k n", k=8) for _ in range(NS)]
        wo = [ar.bf(2 * D).rearrange("p (k n) -> p k n", k=2) for _ in range(1)]
        mrg = [ar.bf(TT) for _ in range(4)]
        sga = [ar.f32(512) for _ in range(1)]
        sgb = [ar.f32(512) for _ in range(1)]
        ucnt = [0]
        wcnt = [0]
        for M in range(NCH // 2):
            ws = 0
            load_w(wout[256 * M:256 * (M + 1), :], wo[ws], ("wo", ws), 2, wscr["wo"][M], wname="mrg")
            for m in (2 * M, 2 * M + 1):
                sl = m % NS
                load_w(win[:, C_GA + 128 * m:C_GA + 128 * (m + 1)], wga[sl], ("wga", sl), NCH, wscr["wga"][m], wname="mrg")
                load_w(win[:, C_GB + 128 * m:C_GB + 128 * (m + 1)], wgb[sl], ("wgb", sl), NCH, wscr["wgb"][m], wname="mrg")
                load_w(wba[:, 128 * m:128 * (m + 1)], wa[sl], ("wa", sl), 8, wscr["wa"][m], wname="mrg")
                load_w(wbb[:, 128 * m:128 * (m + 1)], wb[sl], ("wb", sl), 8, wscr["wb"][m], wname="mrg")
                ms = m % 4
                for si, (c0, n) in enumerate(subs):
                    u = ucnt[0]; ucnt[0] += 1
                    bga, bgb, bba, bbb = (0, 1, 2, 3) if u % 2 == 0 else (4, 5, 6, 7)
                    sa_, sb_ = sga[0], sgb[0]
                    ka_, kb_ = ("sga", 0), ("sgb", 0)
                    for k in range(NCH):
                        PE(lambda h, k=k, sl=sl, c0=c0, n=n, bga=bga: h.matmul(
                            banks[bga][:, 0:n], lhsT=wga[sl][:, k, :], rhs=hbuf[:, k, c0:c0 + n],
                            start=(k == 0), stop=(k == NCH - 1)), [("wga", sl), ("h", k, si)], [("bank", bga)])
                    for k in range(NCH):
                        PE(lambda h, k=k, sl=sl, c0=c0, n=n, bgb=bgb: h.matmul(
                            banks[bgb][:, 0:n], lhsT=wgb[sl][:, k, :], rhs=hbuf[:, k, c0:c0 + n],
                            start=(k == 0), stop=(k == NCH - 1)), [("wgb", sl), ("h", k, si)], [("bank", bgb)])
                    for k in range(8):
                        PE(lambda h, k=k, sl=sl, c0=c0, n=n, bba=bba: h.matmul(
                            banks[bba][:, 0:n], lhsT=wa[sl][:, k, :], rhs=qa[:, k, c0:c0 + n],
                            start=(k == 0), stop=(k == 7)), [("wa", sl), ("qa", k, si)], [("bank", bba)])
                    for k in range(8):
                        PE(lambda h, k=k, sl=sl, c0=c0, n=n, bbb=bbb: h.matmul(
                            banks[bbb][:, 0:n], lhsT=wb[sl][:, k, :], rhs=qb[:, k, c0:c0 + n],
                            start=(k == 0), stop=(k == 7)), [("wb", sl), ("qb", k, si)], [("bank", bbb)])
                    ACT(lambda h, n=n, sa_=sa_, bga=bga: h.activation(out=sa_[:, 0:n], in_=banks[bga][:, 0:n], func=AF.Sigmoid),
                        [("bank", bga)], [ka_])
                    ACT(lambda h, n=n, sb_=sb_, bgb=bgb: h.activation(out=sb_[:, 0:n], in_=banks[bgb][:, 0:n], func=AF.Sigmoid),
                        [("bank", bgb)], [kb_])
                    DVE(lambda h, n=n, sa_=sa_, bba=bba: h.tensor_tensor(out=sa_[:, 0:n], in0=sa_[:, 0:n], in1=banks[bba][:, 0:n],
                                                                        op=ALU.mult), [ka_, ("bank", bba)], [ka_])
                    DVE(lambda h, n=n, sb_=sb_, bbb=bbb: h.tensor_tensor(out=sb_[:, 0:n], in0=sb_[:, 0:n], in1=banks[bbb][:, 0:n],
                                                                        op=ALU.mult), [kb_, ("bank", bbb)], [kb_])
                    DVE(lambda h, ms=ms, c0=c0, n=n, sa_=sa_, sb_=sb_: h.tensor_tensor(
                        out=mrg[ms][:, c0:c0 + n], in0=sa_[:, 0:n], in1=sb_[:, 0:n], op=ALU.add), [ka_, kb_], [("mrg", ms, si)])
            for nn in range(NCH):
                for si, (c0, n) in enumerate(subs):
                    bk = wcnt[0] % 8; wcnt[0] += 1
                    for kk, m in enumerate((2 * M, 2 * M + 1)):
                        PE(lambda h, bk=bk, ws=ws, nn=nn, m=m, kk=kk, c0=c0, n=n: h.matmul(
                            banks[bk][:, 0:n], lhsT=wo[ws][:, kk, nn * 128:(nn + 1) * 128], rhs=mrg[m % 4][:, c0:c0 + n],
                            start=(kk == 0), stop=(kk == 1)), [("wo", ws), ("mrg", m % 4, si)], [("bank", bk)])
                    DVE(lambda h, bk=bk, nn=nn, c0=c0, n=n: h.tensor_tensor(
                        out=x[:, nn, c0:c0 + n], in0=banks[bk][:, 0:n], in1=x[:, nn, c0:c0 + n], op=ALU.add),
                        [("bank", bk), ("x", nn, si)], [("x", nn, si)])

    def final_out(P):
        ar = phase_arena()
        rstd = ar.f32(TT)
        rms_norm_stats(ar, rstd, "rstd", P.subs)
        yt = [ar.f32(NCH * 128).rearrange("p (c t) -> p c t", c=NCH) for _ in range(2)]
        stg = [ar.f32(D) for _ in range(2)]
        for g, (c0, rows) in enumerate(P.tgs):
            y_ = yt[g % 2]
            si = P.si_of(c0)
            for c in range(NCH):
                DVE(lambda h, y_=y_, c=c, c0=c0, rows=rows: h.scalar_tensor_tensor(
                    out=y_[:, c, 0:rows], in0=x[:, c, c0:c0 + rows], scalar=gsb[:, 3, c:c + 1],
                    in1=rstd[:, c0:c0 + rows], op0=ALU.mult, op1=ALU.mult),
                    [("x", c, si), ("rstd", si), ("gsb", 3)], [("yt", g % 2, c)])
            s = stg[g % 2]
            for q in range(4):
                bk = next_bank(list(range(8)))
                for j in range(4):
                    c = q * 4 + j
                    PE(lambda h, y_=y_, c=c, bk=bk, j=j, rows=rows: h.transpose(
                        out=banks[bk][0:rows, j * 128:(j + 1) * 128], in_=y_[:, c, 0:rows], identity=identF),
                       [("yt", g % 2, c), "identF"], [("bank", bk)])
                if q % 2 == 0:
                    ACT(lambda h, s=s, q=q, bk=bk, rows=rows: h.copy(out=s[0:rows, q * 512:(q + 1) * 512],
                                                                    in_=banks[bk][0:rows, :]),
                        [("bank", bk)], [("ystg", g % 2, q)])
                else:
                    DVE(lambda h, s=s, q=q, bk=bk, rows=rows: h.tensor_copy(out=s[0:rows, q * 512:(q + 1) * 512],
                                                                           in_=banks[bk][0:rows, :]),
                        [("bank", bk)], [("ystg", g % 2, q)])
            dst = yp[c0:c0 + rows, :] if P.kind == "own" else ys[g]
            DMA_A(lambda h, s=s, rows=rows, dst=dst: h.dma_start(out=dst, in_=s[0:rows, :]),
                  [("ystg", g % 2, q) for q in range(4)], [], "yout%d" % (g % 2))
            out_streams.add("yout%d" % (g % 2))

    init_consts()
    init_tbx()
    S.barrier()
    passes = [Pass("prefix", 0), Pass("prefix", 1), Pass("prefix", 2), Pass("own"), Pass("sample")]
    for P in passes:
        WMODE.clear()
        if P.kind == "prefix":
            WMODE.update({"f1": "save", "wi_kv": "save"} if P.idx == 0 else {"f1": "scr", "wi_kv": "scr"})
        elif P.kind == "own":
            WMODE.update({"f1": "scr", "wi_kv": "scr", "wi_q": "save", "mrg": "save", "f2": "save"})
        else:
            WMODE.update({"f1": "scr", "wi_kv": "scr", "wi_q": "scr", "mrg": "scr", "f2": "scr"})
        load_x(P)
        S.barrier()
        ffn(f1g, f1u, f1d, 0, P.subs, "f1")
        S.barrier()
        proj_qkv(P)
        S.barrier()
        if P.kind != "prefix":
            attention(P)
            S.barrier()
            merge(P.subs)
            S.barrier()
            ffn(f2g, f2u, f2d, 2, P.subs, "f2")
            S.barrier()
            final_out(P)
        S.barrier(new_epoch=True)

    es = S.emit(nc, sorted(out_streams))
    es.close()
    stack.close()
    return nc


_NC_CACHE = {}


def _rope(pos):
    half = 32
    inv = (10000.0 ** (-np.arange(half, dtype=np.float32) / half)).astype(np.float32)
    pos = np.asarray(pos, np.float32)
    d = np.arange(128) % 64
    ang = pos[None, :] * inv[d % 32][:, None]
    cos = np.cos(ang).astype(np.float32)
    sin = (np.sin(ang) * np.where(d < 32, -1.0, 1.0)[:, None]).astype(np.float32)
    return np.ascontiguousarray(cos), np.ascontiguousarray(sin)


def kernel(x_prompt, x_sample, cache_a_k, cache_a_v, cache_b_k, cache_b_v,
           ffn1_norm, ffn1_w_gate, ffn1_w_up, ffn1_w_down,
           mix_norm, w_in, lambda_q1, lambda_k1, lambda_q2, lambda_k2, subln_a,
           rel_bias_b, w_branch_a, w_branch_b, w_out,
           ffn2_norm, ffn2_w_gate, ffn2_w_up, ffn2_w_down, final_norm):
    f = lambda a: np.ascontiguousarray(np.asarray(a, dtype=np.float32))
    if "nc" not in _NC_CACHE:
        _NC_CACHE["nc"] = build_program()
    nc = _NC_CACHE["nc"]
    xp_all = f(x_prompt); xs_all = f(x_sample)
    cak = f(cache_a_k)[0].reshape(32, 2048, 1024); cav = f(cache_a_v)[0].reshape(32, 2048, 1024)
    cbk = f(cache_b_k)[0].reshape(32, 512, 1024); cbv = f(cache_b_v)[0].reshape(32, 512, 1024)
    gl = [f(ffn1_norm)[0], f(mix_norm)[0], f(ffn2_norm)[0], f(final_norm)]
    cp = [_rope(np.arange(1024 * s, 1024 * (s + 1))) for s in range(3)]
    cS, sS = _rope(2048 + (np.arange(256) % 64))
    shared = {
        "f1g": f(ffn1_w_gate)[0], "f1u": f(ffn1_w_up)[0], "f1d": f(ffn1_w_down)[0],
        "f2g": f(ffn2_w_gate)[0], "f2u": f(ffn2_w_up)[0], "f2d": f(ffn2_w_down)[0],
        "win": f(w_in)[0], "wba": f(w_branch_a)[0], "wbb": f(w_branch_b)[0], "wout": f(w_out)[0],
        "lamv": np.ascontiguousarray(np.broadcast_to(
            np.stack([f(lambda_q1)[0], f(lambda_k1)[0], f(lambda_q2)[0], f(lambda_k2)[0]])[:, None, :], (4, 128, 64))),
        "subln": f(subln_a)[0].reshape(128, 1), "relb": f(rel_bias_b)[0],
        "cosP": np.stack([c for c, _ in cp]), "sinP": np.stack([s_ for _, s_ in cp]),
        "cosS": cS, "sinS": sS, "ident_in": np.eye(128, dtype=np.float32),
    }
    for g in range(4):
        shared["gain%d" % g] = np.ascontiguousarray(gl[g].reshape(NCH, 128).T)
    in_maps = []
    for c in range(NCORES):
        seq, r = c // 4, c % 4
        m = dict(shared)
        m["xpre"] = xp_all[seq, 0:3072].reshape(3, 1024, D)
        m["xown"] = xp_all[seq, 1024 * r:1024 * (r + 1)]
        m["xs"] = xs_all[4 * c:4 * c + 4]
        m["cak"] = cak[4 * c:4 * c + 4]; m["cav"] = cav[4 * c:4 * c + 4]
        m["cbk"] = cbk[4 * c:4 * c + 4]; m["cbv"] = cbv[4 * c:4 * c + 4]
        m["cosO"], m["sinO"] = _rope(np.arange(1024 * r, 1024 * (r + 1)))
        kb = np.full((128, 8), NEG_BIG, np.float32)
        for s_ in range(3):
            if s_ < r:
                kb[:, s_] = 0.0
            if s_ == r - 1:
                kb[:, 3 + s_] = 0.0
        m["kbias"] = kb
        in_maps.append(m)
    res = run_bass_kernel_spmd(nc, in_maps, core_ids=list(range(NCORES)))
    R = res.results
    cat = lambda key, seq: np.concatenate([R[4 * seq + r][key] for r in range(4)], axis=0)
    y_prompt = np.stack([cat("yp", 0), cat("yp", 1)]).reshape(2, 4096, D)
    y_sample = np.concatenate([R[c]["ys"] for c in range(NCORES)]).reshape(32, 64, D)
    akp = np.stack([cat("akp", 0), cat("akp", 1)]).reshape(1, 2, 4096, 8, 128)
    avp = np.stack([cat("avp", 0), cat("avp", 1)]).reshape(1, 2, 4096, 8, 128)
    bkp = np.stack([R[3]["bkp"], R[7]["bkp"]]).reshape(1, 2, 512, 8, 128)
    bvp = np.stack([R[3]["bvp"], R[7]["bvp"]]).reshape(1, 2, 512, 8, 128)
    aks = np.concatenate([R[c]["aks"] for c in range(NCORES)]).reshape(1, 32, 64, 8, 128)
    avs = np.concatenate([R[c]["avs"] for c in range(NCORES)]).reshape(1, 32, 64, 8, 128)
    bks = np.concatenate([R[c]["bks"] for c in range(NCORES)]).reshape(1, 32, 512, 8, 128)
    bvs = np.concatenate([R[c]["bvs"] for c in range(NCORES)]).reshape(1, 32, 512, 8, 128)
    return (y_prompt.astype(np.float32), y_sample.astype(np.float32), akp, avp, bkp, bvp, aks, avs, bks, bvs)
```

```python
import numpy as np
import concourse.bass as bass
import concourse.mybir as mybir
from concourse.bass_utils import run_bass_kernel_spmd

F32 = mybir.dt.float32
BF16 = mybir.dt.bfloat16
AF = mybir.ActivationFunctionType
ALU = mybir.AluOpType
AX = mybir.AxisListType

D = 2048
FF = 8192
NCH = 16
TP = 1024
TSM = 64
TT = 1024
NTILES = 4
SUB = [(0, 512), (512, 512), (1024, 64)]
TG = [(g * 128, 128) for g in range(8)] + [(1024, 64)]
EPS = 1e-6
LAM_INIT = 0.2
NEG_BIG = -300.0
C_QA, C_KA, C_VA, C_QB, C_KB, C_VB, C_GA, C_GB = 0, 1024, 2048, 3072, 4096, 5120, 6144, 8192
NCORES = 8


class Op:
    __slots__ = ("eng", "fn", "deps", "dma", "sig", "epoch")

    def __init__(self, eng, fn, deps, dma, epoch):
        self.eng, self.fn, self.deps, self.dma, self.epoch = eng, fn, deps, dma, epoch
        self.sig = None


class Sched:
    ENGS = ("pe", "act", "dve", "pool", "sp")

    def __init__(self):
        self.ops = []
        self.last_w = {}
        self.readers = {}
        self.epoch = 0
        self.barrier_deps = {e: set() for e in self.ENGS}
        self.last_on_eng = {}
        self.last_on_stream = {}

    def add(self, eng, fn, r=(), w=(), dma=None):
        deps = set()
        for k in r:
            if k in self.last_w:
                deps.add(self.last_w[k])
        for k in w:
            if k in self.last_w:
                deps.add(self.last_w[k])
            deps.update(self.readers.get(k, ()))
        if self.barrier_deps[eng]:
            deps.update(self.barrier_deps[eng])
            self.barrier_deps[eng] = set()
        idx = len(self.ops)
        self.ops.append(Op(eng, fn, deps, dma, self.epoch))
        for k in r:
            self.readers.setdefault(k, []).append(idx)
        for k in w:
            self.last_w[k] = idx
            self.readers[k] = []
        if dma is None:
            self.last_on_eng[eng] = idx
        else:
            self.last_on_stream[dma] = idx
        return idx

    def barrier(self, new_epoch=False):
        s = set(self.last_on_eng.values()) | set(self.last_on_stream.values())
        for e in self.ENGS:
            self.barrier_deps[e] = set(s)
        if new_epoch:
            self.epoch += 1

    def emit(self, nc, final_streams):
        ops = self.ops
        need = set()
        for op in ops:
            need.update(op.deps)
        streams = sorted({op.dma for op in ops if op.dma is not None})
        nep = self.epoch + 1
        import contextlib
        es = contextlib.ExitStack()
        sem_stream = {s: es.enter_context(nc.semaphore("st_" + s)) for s in streams}
        sem_eng = {(e, ep): es.enter_context(nc.semaphore("e_%s_%d" % (e, ep)))
                   for e in ("pe", "act", "dve") for ep in range(nep)}
        block = es.enter_context(nc.Block())
        cnt_s = {s: 0 for s in streams}
        cnt_e = {k: 0 for k in sem_eng}
        for i, op in enumerate(ops):
            if op.dma is not None:
                cnt_s[op.dma] += 16
                op.sig = (sem_stream[op.dma], cnt_s[op.dma], 16)
            elif i in need:
                k = (op.eng, op.epoch)
                cnt_e[k] += 1
                op.sig = (sem_eng[k], cnt_e[k], 1)
        per = {e: [] for e in self.ENGS}
        for i, op in enumerate(ops):
            per[op.eng].append(i)

        def run(eng_name, h):
            waited = {}
            for i in per[eng_name]:
                op = ops[i]
                wl = {}
                for d in op.deps:
                    dop = ops[d]
                    if dop.dma is None and dop.eng == eng_name and eng_name == "pe":
                        continue
                    sem, val, _ = dop.sig
                    key = id(sem)
                    if waited.get(key, 0) >= val:
                        continue
                    if key not in wl or wl[key][1] < val:
                        wl[key] = (sem, val)
                for key, (sem, val) in wl.items():
                    h.wait_ge(sem, val)
                    waited[key] = val
                ins = op.fn(h)
                if op.sig is not None:
                    assert ins is not None, (eng_name, i)
                    ins.then_inc(op.sig[0], op.sig[2])
            if eng_name == "sp":
                for s in final_streams:
                    if s in cnt_s and cnt_s[s] > 0:
                        h.wait_ge(sem_stream[s], cnt_s[s])

        @block.tensor
        def _(h):
            run("pe", h)

        @block.scalar
        def _(h):
            run("act", h)

        @block.vector
        def _(h):
            run("dve", h)

        @block.gpsimd
        def _(h):
            run("pool", h)

        @block.sync
        def _(h):
            run("sp", h)
        return es


def build_program():
    nc = bass.Bass("TRN2", target_bir_lowering=False)
    S = Sched()

    def din(name, shape, dt=F32):
        return nc.dram_tensor(name, list(shape), dt, kind="ExternalInput").ap()

    def dout(name, shape, dt=F32):
        return nc.dram_tensor(name, list(shape), dt, kind="ExternalOutput").ap()

    def dscr(name, shape, dt):
        return nc.dram_tensor(name, list(shape), dt, kind="Internal")

    xpre = din("xpre", [3, 1024, D]); xown = din("xown", [1024, D]); xs = din("xs", [4, 64, D])
    cak = din("cak", [4, 2048, 1024]); cav = din("cav", [4, 2048, 1024])
    cbk = din("cbk", [4, 512, 1024]); cbv = din("cbv", [4, 512, 1024])
    f1g = din("f1g", [D, FF]); f1u = din("f1u", [D, FF]); f1d = din("f1d", [FF, D])
    f2g = din("f2g", [D, FF]); f2u = din("f2u", [D, FF]); f2d = din("f2d", [FF, D])
    win = din("win", [D, 10240]); wba = din("wba", [1024, D]); wbb = din("wbb", [1024, D]); wout = din("wout", [D, D])
    lamv = din("lamv", [4, 128, 64]); subln = din("subln", [128, 1]); relb = din("relb", [8, 257])
    cosP = din("cosP", [3, 128, 1024]); sinP = din("sinP", [3, 128, 1024])
    cosO = din("cosO", [128, 1024]); sinO = din("sinO", [128, 1024])
    cosS = din("cosS", [128, 256]); sinS = din("sinS", [128, 256])
    kbias_in = din("kbias", [128, 8])
    yp = dout("yp", [1024, D]); ys = dout("ys", [4, 64, D])
    akp = dout("akp", [1024, 1024]); avp = dout("avp", [1024, 1024])
    bkp = dout("bkp", [512, 1024]); bvp = dout("bvp", [512, 1024])
    aks = dout("aks", [4, 64, 1024]); avs = dout("avs", [4, 64, 1024])
    bks = dout("bks", [4, 512, 1024]); bvs = dout("bvs", [4, 512, 1024])
    kaT_s = dscr("kaT_s", [8, 128, 4096], BF16).ap()
    kbT_s = dscr("kbT_s", [8, 128, 4096], BF16).ap()
    va_s = dscr("va_s", [8, 128, 32, 128], BF16).ap()
    vb_s = dscr("vb_s", [8, 128, 32, 128], BF16).ap()
    tbx_t = dscr("tbx", [8, 128, 1024], F32)
    wscr = {
        "f1g": dscr("f1g_s", [32, 128, 4096], BF16).ap(), "f1u": dscr("f1u_s", [32, 128, 4096], BF16).ap(),
        "f1d": dscr("f1d_s", [32, 128, 4096], BF16).ap(),
        "f2g": dscr("f2g_s", [32, 128, 4096], BF16).ap(), "f2u": dscr("f2u_s", [32, 128, 4096], BF16).ap(),
        "f2d": dscr("f2d_s", [32, 128, 4096], BF16).ap(),
        "wi": dscr("wi_s", [24, 128, 4096], BF16).ap(),
        "wga": dscr("wga_s", [16, 128, 2048], BF16).ap(), "wgb": dscr("wgb_s", [16, 128, 2048], BF16).ap(),
        "wa": dscr("wa_s", [16, 128, 1024], BF16).ap(), "wb": dscr("wb_s", [16, 128, 1024], BF16).ap(),
        "wo": dscr("wo_s", [8, 128, 4096], BF16).ap(),
    }
    tbx = tbx_t.ap()

    import contextlib
    stack = contextlib.ExitStack()
    BIGW = 53200
    big = stack.enter_context(nc.sbuf_tensor("big", [128, BIGW], F32))
    banks = [stack.enter_context(nc.psum_tensor("bank%d" % i, [128, 512], F32)) for i in range(8)]

    class Arena:
        def __init__(self, base, limit):
            self.p, self.limit = base, limit

        def f32(self, n):
            o = self.p; self.p += n
            assert self.p <= self.limit, (self.p, self.limit)
            return big[:, o:o + n]

        def bf(self, n):
            n4 = (n + 1) // 2
            o = self.p; self.p += n4
            assert self.p <= self.limit, (self.p, self.limit)
            return big[:, o:o + n4].bitcast(BF16)[:, 0:n]

    A0 = Arena(0, BIGW)
    x_flat = A0.f32(NCH * TT)
    x = x_flat.rearrange("p (c t) -> p c t", c=NCH)
    identF = A0.f32(128)
    identB = A0.bf(128)
    onesB = A0.bf(128)
    onesF = A0.f32(128)
    gsb = A0.f32(4 * NCH).rearrange("p (g c) -> p g c", g=4)
    gsub = A0.f32(1)
    neglam = A0.f32(1)
    epsT = A0.f32(1)
    rstd_mix = A0.f32(TT)
    ksTa = A0.bf(8 * 256).rearrange("p (h t) -> p h t", h=8)
    ksTb = A0.bf(8 * 256).rearrange("p (h t) -> p h t", h=8)
    vsa = A0.bf(4 * 1024).rearrange("p (b f) -> p b f", b=4)
    vsb = A0.bf(4 * 1024).rearrange("p (b f) -> p b f", b=4)
    kbias = A0.f32(8)
    lamtmp = A0.f32(4 * 64).rearrange("p (a b) -> p a b", a=4)
    lamr = A0.f32(4)
    PBASE = A0.p
    QO = Arena(PBASE, BIGW)
    qa = QO.bf(8 * TT).rearrange("p (c t) -> p c t", c=8)
    qb = QO.bf(8 * TT).rearrange("p (c t) -> p c t", c=8)
    QBASE = QO.p

    def phase_arena(keep_qo=False):
        return Arena(QBASE if keep_qo else PBASE, BIGW)

    def PE(fn, r, w): return S.add("pe", fn, r, w)
    def ACT(fn, r, w): return S.add("act", fn, r, w)
    def DVE(fn, r, w): return S.add("dve", fn, r, w)
    def DMA_P(fn, r, w, stream): return S.add("pool", fn, r, w, dma=stream)
    def DMA_S(fn, r, w, stream): return S.add("sp", fn, r, w, dma=stream)
    def DMA_A(fn, r, w, stream): return S.add("act", fn, r, w, dma=stream)
    out_streams = set()

    bank_rr = [0]

    def next_bank(pool):
        b = pool[bank_rr[0] % len(pool)]
        bank_rr[0] += 1
        return b

    def init_consts():
        DVE(lambda h: h.memset(onesF, 1.0), [], ["onesF"])
        DVE(lambda h: h.memset(onesB, 1.0), [], ["onesB"])
        DVE(lambda h: h.memset(epsT, EPS), [], ["epsT"])
        DMA_S(lambda h: h.dma_start(out=identF, in_=ident_in), [], ["identF"], "cst")
        for g in range(4):
            DMA_S(lambda h, g=g: h.dma_start(out=gsb[:, g, :], in_=gains_r[g]), [], [("gsb", g)], "cst")
        DMA_S(lambda h: h.dma_start(out=gsub, in_=subln), [], ["gsub"], "cst")
        DMA_S(lambda h: h.dma_start(out=kbias, in_=kbias_in), [], ["kbias"], "cst")
        for a in range(4):
            DMA_S(lambda h, a=a: h.dma_start(out=lamtmp[:, a, :], in_=lamv[a]), [], [("lamtmp", a)], "cst")
        S.barrier()
        DVE(lambda h: h.tensor_copy(out=identB, in_=identF), ["identF"], ["identB"])
        DVE(lambda h: h.tensor_scalar(out=gsub, in0=gsub, scalar1=1.0 - LAM_INIT, scalar2=0.0, op0=ALU.mult, op1=ALU.add),
            ["gsub"], ["gsub"])
        DVE(lambda h: h.tensor_tensor(out=lamtmp[:, 0, :], in0=lamtmp[:, 0, :], in1=lamtmp[:, 1, :], op=ALU.mult),
            [("lamtmp", 0), ("lamtmp", 1)], [("lamtmp", 0)])
        DVE(lambda h: h.tensor_tensor(out=lamtmp[:, 2, :], in0=lamtmp[:, 2, :], in1=lamtmp[:, 3, :], op=ALU.mult),
            [("lamtmp", 2), ("lamtmp", 3)], [("lamtmp", 2)])
        DVE(lambda h: h.reduce_sum(out=lamr[:, 0:1], in_=lamtmp[:, 0, :], axis=AX.X), [("lamtmp", 0)], [("lamr", 0)])
        DVE(lambda h: h.reduce_sum(out=lamr[:, 1:2], in_=lamtmp[:, 2, :], axis=AX.X), [("lamtmp", 2)], [("lamr", 1)])
        ACT(lambda h: h.activation(out=lamr[:, 2:4], in_=lamr[:, 0:2], func=AF.Exp), [("lamr", 0), ("lamr", 1)], [("lamr", 2)])
        DVE(lambda h: h.tensor_tensor(out=neglam, in0=lamr[:, 3:4], in1=lamr[:, 2:3], op=ALU.subtract),
            [("lamr", 2)], ["neglam"])
        DVE(lambda h: h.tensor_scalar(out=neglam, in0=neglam, scalar1=-LAM_INIT, scalar2=0.0, op0=ALU.add, op1=ALU.add),
            ["neglam"], ["neglam"])

    ident_in = din("ident_in", [128, 128])
    gains_r = [din("gain%d" % g, [128, NCH]) for g in range(4)]

    def init_tbx():
        ar = phase_arena()
        ext = ar.f32(768)
        DMA_S(lambda h: h.dma_start(out=ext[0:8, 0:257], in_=relb), [], ["ext"], "cst")
        S.barrier()
        DVE(lambda h: h.memset(ext[0:8, 257:768], 0.0), [], ["ext2"])
        DVE(lambda h: h.tensor_scalar(out=ext[0:8, 257:768], in0=ext[0:8, 257:768], scalar1=ext[0:8, 256:257], scalar2=0.0, op0=ALU.add, op1=ALU.add), ["ext", "ext2"], ["ext"])
        for kk in range(128):
            DMA_S(lambda h, kk=kk: h.dma_start(out=tbx[:, kk, 0:768], in_=ext[0:8, :]), ["ext"], ["tbx"], "tbx")

    class Pass:
        def __init__(self, kind, idx=0):
            self.kind, self.idx = kind, idx
            if kind == "sample":
                self.subs = [(0, 256)]
                self.tgs = [(64 * b, 64) for b in range(4)]
                self.W = 256
            else:
                self.subs = [(0, 512), (512, 512)]
                self.tgs = [(128 * g, 128) for g in range(8)]
                self.W = 1024
            self.slot = idx if kind == "prefix" else 3

        def si_of(self, c0):
            for si, (a, n) in enumerate(self.subs):
                if a <= c0 < a + n:
                    return si
            raise AssertionError(c0)

    WMODE = {}

    def load_w(dram2d, slot_ap, key, kparts, scr=None, defer_save=False, wname=None):
        mode = WMODE.get(wname, "cast") if scr is not None else "cast"
        if mode == "scr":
            DMA_S(lambda h: h.dma_start(out=slot_ap, in_=scr.rearrange("p (k n) -> p k n", k=kparts)), [], [key],
                  "w_%s_%d" % key)
            return
        src = dram2d.rearrange("(k p) n -> p k n", p=128)
        DMA_P(lambda h: h.dma_start(out=slot_ap, in_=src), [], [key], "w_%s_%d" % key)
        if mode == "save":
            def do_save():
                DMA_S(lambda h: h.dma_start(out=scr.rearrange("p (k n) -> p k n", k=kparts), in_=slot_ap), [key], [],
                      "ws_%s_%d" % key)
            if defer_save:
                return do_save
            do_save()
        return None

    def rms_norm_stats(ar, rstd_out, rstd_key, subs, rs=None):
        sq = [ar.bf(512) for _ in range(2)]
        if rs is None:
            rs = ar.f32(512)
        for si, (c0, n) in enumerate(subs):
            bk = next_bank(list(range(8)))
            for c in range(NCH):
                s = sq[c % 2]
                ACT(lambda h, s=s, c=c, c0=c0, n=n: h.activation(out=s[:, 0:n], in_=x[:, c, c0:c0 + n], func=AF.Square),
                    [("x", c, si)], [("sq", c % 2)])
                PE(lambda h, s=s, c=c, bk=bk, n=n: h.matmul(banks[bk][:, 0:n], lhsT=onesB, rhs=s[:, 0:n],
                                                          start=(c == 0), stop=(c == NCH - 1)),
                   [("sq", c % 2), "onesB"], [("bank", bk)])
            ACT(lambda h, bk=bk, n=n: h.activation(out=rs[:, 0:n], in_=banks[bk][:, 0:n], func=AF.Sqrt,
                                                   bias=epsT, scale=1.0 / D),
                [("bank", bk), "epsT"], ["rs"])
            DVE(lambda h, c0=c0, n=n: h.reciprocal(out=rstd_out[:, c0:c0 + n], in_=rs[:, 0:n]), ["rs"], [(rstd_key, si)])

    def apply_norm(hbuf, rstd_ap, rstd_key, gidx, subs):
        for si, (c0, n) in enumerate(subs):
            for c in range(NCH):
                DVE(lambda h, c=c, c0=c0, n=n: h.scalar_tensor_tensor(
                    out=hbuf[:, c, c0:c0 + n], in0=x[:, c, c0:c0 + n], scalar=gsb[:, gidx, c:c + 1],
                    in1=rstd_ap[:, c0:c0 + n], op0=ALU.mult, op1=ALU.mult),
                    [("x", c, si), (rstd_key, si), ("gsb", gidx)], [("h", c, si)])

    def load_x(P):
        ar = phase_arena()
        stg = [ar.f32(D) for _ in range(2)]
        for g, (c0, rows) in enumerate(P.tgs):
            s = stg[g % 2]
            if P.kind == "prefix":
                src = xpre[P.idx, c0:c0 + rows, :]
            elif P.kind == "own":
                src = xown[c0:c0 + rows, :]
            else:
                src = xs[g]
            DMA_S(lambda h, s=s, src=src, rows=rows: h.dma_start(out=s[0:rows, :], in_=src), [], [("xstg", g % 2)],
                  "xin%d" % (g % 2))
            si = P.si_of(c0)
            for q in range(4):
                bk = next_bank(list(range(8)))
                for j in range(4):
                    c = q * 4 + j
                    PE(lambda h, s=s, c=c, bk=bk, j=j, rows=rows: h.transpose(
                        out=banks[bk][:, j * 128: j * 128 + rows], in_=s[0:rows, c * 128:(c + 1) * 128],
                        identity=identF[0:rows, 0:rows]),
                       [("xstg", g % 2), "identF"], [("bank", bk)])
                outv = x[:, q * 4:(q + 1) * 4, c0:c0 + rows]
                inv = banks[bk][:].rearrange("p (j t) -> p j t", j=4)[:, :, 0:rows]
                wkeys = [("x", q * 4 + j, si) for j in range(4)]
                if q % 2 == 0:
                    ACT(lambda h, outv=outv, inv=inv: h.copy(out=outv, in_=inv), [("bank", bk)], wkeys)
                else:
                    DVE(lambda h, outv=outv, inv=inv: h.tensor_copy(out=outv, in_=inv), [("bank", bk)], wkeys)

    def ffn(wg, wu, wd, gidx, subs, sn):
        ar = phase_arena()
        hbuf = ar.bf(NCH * TT).rearrange("p (c t) -> p c t", c=NCH)
        rstd = ar.f32(TT)
        rms_norm_stats(ar, rstd, "rstd", subs)
        apply_norm(hbuf, rstd, "rstd", gidx, subs)
        NG = FF // 256
        NS = 2
        ND = 3
        wgs = [ar.bf(NCH * 256).rearrange("p (k n) -> p k n", k=NCH) for _ in range(NS)]
        wus = [ar.bf(NCH * 256).rearrange("p (k n) -> p k n", k=NCH) for _ in range(NS)]
        wds = [ar.bf(2 * D).rearrange("p (k n) -> p k n", k=2) for _ in range(ND)]
        hid = [ar.bf(2 * TT).rearrange("p (k t) -> p k t", k=2) for _ in range(ND)]
        sg = [ar.f32(512) for _ in range(2)]
        gu_banks = [0, 1, 2, 3]
        d_banks = [4, 5, 6, 7]
        cnt = [0, 0]

        def gu(g):
            sl = g % NS
            ds_ = g % ND
            load_w(wg[:, g * 256:(g + 1) * 256], wgs[sl], ("wg", sl), NCH, wscr[sn + "g"][g], wname=sn)
            load_w(wu[:, g * 256:(g + 1) * 256], wus[sl], ("wu", sl), NCH, wscr[sn + "u"][g], wname=sn)
            load_w(wd[g * 256:(g + 1) * 256, :], wds[ds_], ("wd", ds_), 2, wscr[sn + "d"][g], wname=sn)
            hs = g % ND
            for cc in range(2):
                for si, (c0, n) in enumerate(subs):
                    u = cnt[0]; cnt[0] += 1
                    bg = gu_banks[(2 * u) % 4]; bu = gu_banks[(2 * u + 1) % 4]
                    for k in range(NCH):
                        PE(lambda h, k=k, bg=bg, sl=sl, cc=cc, c0=c0, n=n: h.matmul(
                            banks[bg][:, 0:n], lhsT=wgs[sl][:, k, cc * 128:(cc + 1) * 128], rhs=hbuf[:, k, c0:c0 + n],
                            start=(k == 0), stop=(k == NCH - 1)),
                           [("wg", sl), ("h", k, si)], [("bank", bg)])
                    for k in range(NCH):
                        PE(lambda h, k=k, bu=bu, sl=sl, cc=cc, c0=c0, n=n: h.matmul(
                            banks[bu][:, 0:n], lhsT=wus[sl][:, k, cc * 128:(cc + 1) * 128], rhs=hbuf[:, k, c0:c0 + n],
                            start=(k == 0), stop=(k == NCH - 1)),
                           [("wu", sl), ("h", k, si)], [("bank", bu)])
                    s = sg[u % 2]
                    ACT(lambda h, s=s, bg=bg, n=n: h.activation(out=s[:, 0:n], in_=banks[bg][:, 0:n], func=AF.Silu),
                        [("bank", bg)], [("sg", u % 2)])
                    DVE(lambda h, s=s, bu=bu, hs=hs, cc=cc, c0=c0, n=n: h.tensor_tensor(
                        out=hid[hs][:, cc, c0:c0 + n], in0=s[:, 0:n], in1=banks[bu][:, 0:n], op=ALU.mult),
                        [("sg", u % 2), ("bank", bu)], [("hid", hs, cc, si)])

        def down2(G):
            gs = (2 * G, 2 * G + 1)
            for m in range(NCH):
                for si, (c0, n) in enumerate(subs):
                    u = cnt[1]; cnt[1] += 1
                    bd = d_banks[u % 4]
                    idx = 0
                    for g in gs:
                        for cc in range(2):
                            PE(lambda h, cc=cc, bd=bd, g=g, m=m, c0=c0, n=n, idx=idx: h.matmul(
                                banks[bd][:, 0:n], lhsT=wds[g % ND][:, cc, m * 128:(m + 1) * 128],
                                rhs=hid[g % ND][:, cc, c0:c0 + n], start=(idx == 0), stop=(idx == 3)),
                               [("wd", g % ND), ("hid", g % ND, cc, si)], [("bank", bd)])
                            idx += 1
                    DVE(lambda h, bd=bd, m=m, c0=c0, n=n: h.scalar_tensor_tensor(
                        out=x[:, m, c0:c0 + n], in0=banks[bd][:, 0:n], scalar=0.5, in1=x[:, m, c0:c0 + n],
                        op0=ALU.mult, op1=ALU.add),
                        [("bank", bd), ("x", m, si)], [("x", m, si)])

        gu(0)
        gu(1)
        for G in range(NG // 2):
            if 2 * G + 2 < NG:
                gu(2 * G + 2)
            down2(G)
            if 2 * G + 3 < NG:
                gu(2 * G + 3)

    def proj_qkv(P):
        ar = phase_arena(True)
        W = P.W
        hbuf = ar.bf(NCH * TT).rearrange("p (c t) -> p c t", c=NCH)
        t1 = ar.f32(TT)
        rms_norm_stats(ar, rstd_mix, "rstd_mix", P.subs, rs=t1[:, 0:512])
        apply_norm(hbuf, rstd_mix, "rstd_mix", 1, P.subs)
        cosb = ar.f32(TT); sinb = ar.f32(TT)
        if P.kind == "prefix":
            csrc, ssrc = cosP[P.idx], sinP[P.idx]
        elif P.kind == "own":
            csrc, ssrc = cosO, sinO
        else:
            csrc, ssrc = cosS, sinS
        DMA_S(lambda h: h.dma_start(out=cosb[:, 0:W], in_=csrc), [], ["cos"], "cs_c")
        DMA_S(lambda h: h.dma_start(out=sinb[:, 0:W], in_=ssrc), [], ["sin"], "cs_s")
        NS = 2
        wsl = [ar.bf(NCH * 256).rearrange("p (k n) -> p k n", k=NCH) for _ in range(NS)]
        zf = [ar.f32(TT) for _ in range(2)]
        zs1 = ar.f32(TT)
        kst = [ar.f32(512).rearrange("p (g f) -> p g f", g=4) for _ in range(2)]
        vst = [ar.f32(256) for _ in range(3)]
        gcount = [0]
        ucount = [0]
        kcount = [0]
        vcount = [0]

        pending = []

        def flush_pending():
            while pending:
                pending.pop(0)()

        def k_outputs(src, src_key, hd, is_a):
            if P.kind != "sample":
                ks = kaT_s if is_a else kbT_s
                DMA_P(lambda h: h.dma_start(out=ks[hd, :, P.slot * 1024:(P.slot + 1) * 1024], in_=src[:, 0:1024]),
                      [src_key], [("kscr", is_a, P.slot)], "kscr")
            else:
                kst_small = ksTa if is_a else ksTb
                DVE(lambda h: h.tensor_copy(out=kst_small[:, hd, :], in_=src[:, 0:256]), [src_key], [("kst", is_a, hd)])
            if P.kind == "prefix":
                return
            pending.append(lambda: k_out_rows(src, src_key, hd, is_a))

        def k_out_rows(src, src_key, hd, is_a):
            if P.kind == "own":
                groups = list(range(8)) if is_a else [4, 5, 6, 7]
            else:
                groups = [0, 1, 2, 3]
            for b0 in range(0, len(groups), 4):
                gl = groups[b0:b0 + 4]
                bk = next_bank(list(range(8)))
                for j, g in enumerate(gl):
                    c0, rows = P.tgs[g]
                    PE(lambda h, j=j, c0=c0, rows=rows, bk=bk: h.transpose(
                        out=banks[bk][0:rows, j * 128:(j + 1) * 128], in_=src[:, c0:c0 + rows], identity=identF),
                       [src_key, "identF"], [("bank", bk)])
                kc_ = kcount[0]; kcount[0] += 1
                st = kst[kc_ % 2]
                ACT(lambda h, st=st, bk=bk: h.copy(out=st, in_=banks[bk][:].rearrange("p (g f) -> p g f", g=4)),
                    [("bank", bk)], [("kst_stage", kc_ % 2)])
                for j, g in enumerate(gl):
                    c0, rows = P.tgs[g]
                    hs_ = slice(hd * 128, (hd + 1) * 128)
                    if P.kind == "own":
                        dst = akp[c0:c0 + rows, hs_] if is_a else bkp[c0 - 512:c0 - 512 + rows, hs_]
                    else:
                        dst = aks[g, :, hs_] if is_a else bks[g, 448:512, hs_]
                    DMA_A(lambda h, st=st, j=j, rows=rows, dst=dst: h.dma_start(out=dst, in_=st[0:rows, j, :]),
                          [("kst_stage", kc_ % 2)], [], "kout%d" % (kc_ % 2))
                    out_streams.add("kout%d" % (kc_ % 2))

        def feat_group(colbase, kind, hd0, sl):
            for cc in range(2):
                hd = hd0 + cc
                u = ucount[0]; ucount[0] += 1
                z = zf[u % 2]
                for si, (c0, n) in enumerate(P.subs):
                    bk = next_bank(list(range(8)))
                    for k in range(NCH):
                        PE(lambda h, k=k, bk=bk, cc=cc, c0=c0, n=n: h.matmul(
                            banks[bk][:, 0:n], lhsT=wsl[sl][:, k, cc * 128:(cc + 1) * 128], rhs=hbuf[:, k, c0:c0 + n],
                            start=(k == 0), stop=(k == NCH - 1)),
                           [("wi", sl), ("h", k, si)], [("bank", bk)])
                    if kind == "qb":
                        ACT(lambda h, bk=bk, hd=hd, c0=c0, n=n: h.copy(out=qb[:, hd, c0:c0 + n], in_=banks[bk][:, 0:n]),
                            [("bank", bk)], [("qb", hd, si)])
                    else:
                        ACT(lambda h, bk=bk, z=z, c0=c0, n=n: h.copy(out=z[:, c0:c0 + n], in_=banks[bk][:, 0:n]),
                            [("bank", bk)], [("zf", u % 2)])
                flush_pending()
                if kind == "qb":
                    continue
                if kind == "kb":
                    k_outputs(z, ("zf", u % 2), hd, False)
                    continue
                zz = zs1
                for blk in range(4):
                    p0 = blk * 32
                    p1 = p0 ^ 32
                    DMA_A(lambda h, z=z, zz=zz, p0=p0, p1=p1: h.dma_start(out=zz[p0:p0 + 32, 0:W], in_=z[p1:p1 + 32, 0:W]),
                          [("zf", u % 2)], ["zs"], "swap")
                DVE(lambda h, z=z: h.tensor_tensor(out=t1[:, 0:W], in0=z[:, 0:W], in1=cosb[:, 0:W], op=ALU.mult),
                    [("zf", u % 2), "cos"], ["t1"])
                DVE(lambda h, zz=zz: h.tensor_tensor(out=zz[:, 0:W], in0=zz[:, 0:W], in1=sinb[:, 0:W], op=ALU.mult),
                    ["zs", "sin"], ["zs"])
                if kind == "qa":
                    DVE(lambda h, zz=zz, hd=hd: h.tensor_tensor(out=qa[:, hd, 0:W], in0=t1[:, 0:W], in1=zz[:, 0:W], op=ALU.add),
                        ["t1", "zs"], [("qa", hd, si_) for si_ in range(len(P.subs))])
                else:
                    DVE(lambda h, zz=zz, z=z: h.tensor_tensor(out=z[:, 0:W], in0=t1[:, 0:W], in1=zz[:, 0:W], op=ALU.add),
                        ["t1", "zs"], [("zf", u % 2)])
                    k_outputs(z, ("zf", u % 2), hd, True)

        def tok_group(colbase, is_a, hd0, sl):
            vs = va_s if is_a else vb_s
            cs = slice(hd0 * 128, hd0 * 128 + 256)
            for g, (c0, rows) in enumerate(P.tgs):
                bk = next_bank(list(range(8)))
                for k in range(NCH):
                    PE(lambda h, k=k, bk=bk, c0=c0, rows=rows: h.matmul(
                        banks[bk][0:rows, 0:256], lhsT=hbuf[:, k, c0:c0 + rows], rhs=wsl[sl][:, k, :],
                        start=(k == 0), stop=(k == NCH - 1)),
                       [("wi", sl), ("h", k, P.si_of(c0))], [("bank", bk)])
                vc_ = vcount[0]; vcount[0] += 1
                st = vst[vc_ % 3]
                ACT(lambda h, st=st, bk=bk, rows=rows: h.copy(out=st[0:rows, :], in_=banks[bk][0:rows, 0:256]),
                    [("bank", bk)], [("vst", vc_ % 3)])
                if g == 0:
                    flush_pending()
                dsts = []
                if P.kind != "sample":
                    kt = P.slot * 8 + g
                    DMA_P(lambda h, st=st, kt=kt: h.dma_start(
                        out=vs[hd0:hd0 + 2, :, kt, :].rearrange("h p d -> p h d"),
                        in_=st.rearrange("p (h d) -> p h d", h=2)),
                        [("vst", vc_ % 3)], [("vscr", is_a, P.slot)], "vscr")
                    if P.kind == "own":
                        if is_a:
                            dsts = [avp[c0:c0 + rows, cs]]
                        elif g >= 4:
                            dsts = [bvp[c0 - 512:c0 - 512 + rows, cs]]
                else:
                    dsts = [avs[g, :, cs] if is_a else bvs[g, 448:512, cs]]
                    vsm = vsa if is_a else vsb
                    DVE(lambda h, st=st, vsm=vsm, g=g: h.tensor_copy(out=vsm[0:64, g, cs], in_=st[0:64, :]),
                        [("vst", vc_ % 3)], [("vsm", is_a, hd0, g)])
                for dst in dsts:
                    DMA_A(lambda h, st=st, rows=rows, dst=dst: h.dma_start(out=dst, in_=st[0:rows, :]),
                          [("vst", vc_ % 3)], [], "vout%d" % (vc_ % 3))
                    out_streams.add("vout%d" % (vc_ % 3))

        ka_g = [("f", C_KA + 256 * j, "ka", 2 * j) for j in range(4)]
        va_g = [("t", C_VA + 256 * j, True, 2 * j) for j in range(4)]
        kb_g = [("f", C_KB + 256 * j, "kb", 2 * j) for j in range(4)]
        vb_g = [("t", C_VB + 256 * j, False, 2 * j) for j in range(4)]
        qa_g = [("f", C_QA + 256 * j, "qa", 2 * j) for j in range(4)]
        qb_g = [("f", C_QB + 256 * j, "qb", 2 * j) for j in range(4)]
        glist = []
        if P.kind != "prefix":
            for j in range(4):
                glist += [qa_g[j], qb_g[j]]
        for j in range(4):
            glist += [ka_g[j], va_g[j]]
        for j in range(4):
            glist += [kb_g[j], vb_g[j]]

        saves = {}

        def issue_load(i):
            colbase = glist[i][1]
            wn = "wi_q" if glist[i][2] in ("qa", "qb") else "wi_kv"
            saves[i] = load_w(win[:, colbase:colbase + 256], wsl[i % NS], ("wi", i % NS), NCH,
                              wscr["wi"][colbase // 256], defer_save=True, wname=wn)

        issue_load(0)
        for i, (typ, colbase, a3, hd0) in enumerate(glist):
            if i + 1 < len(glist):
                issue_load(i + 1)
            if typ == "f":
                feat_group(colbase, a3, hd0, i % NS)
            else:
                tok_group(colbase, a3, hd0, i % NS)
            if saves.get(i) is not None:
                saves[i]()
        flush_pending()
        if P.kind == "sample":
            for b in range(4):
                DMA_S(lambda h, b=b: h.dma_start(out=bks[b, 0:448, :], in_=cbk[b, 64:512, :]), [], [], "roll")
                DMA_S(lambda h, b=b: h.dma_start(out=bvs[b, 0:448, :], in_=cbv[b, 64:512, :]), [], [], "roll")
            out_streams.add("roll")

    def attention(P):
        ar = phase_arena(True)
        kT = [ar.bf(4096) for _ in range(2)]
        vv = [ar.bf(32 * 128).rearrange("p (k d) -> p k d", k=32) for _ in range(2)]
        Pt = [ar.bf(512) for _ in range(6)] if P.kind == "own" else None
        ef = [ar.f32(512) for _ in range(4)]
        tb = ar.f32(8 * 640).rearrange("p (h q) -> p h q", h=8)
        src = bass.AP(tbx_t, 128, [[1023, 128], [128 * 1024, 8], [1, 640]])
        DMA_S(lambda h: h.dma_start(out=tb, in_=src), ["tbx"], ["tb"], "tbl")
        pcnt = [0]
        ecnt = [0]
        SCL_A = 64 ** -0.5
        SCL_B = 128 ** -0.5
        allk = lambda is_a: [("kscr", is_a, t) for t in range(4)]
        allv = lambda is_a: [("vscr", is_a, t) for t in range(4)]

        def epilogue_a(Ob, Db, width, out_ap, out_keys):
            e0, e1, e2, e3 = [ef[(ecnt[0] + j) % 4] for j in range(4)]
            k0, k1, k2, k3 = [("ef", (ecnt[0] + j) % 4) for j in range(4)]
            ecnt[0] += 4
            DVE(lambda h: h.reciprocal(out=e0[:, 0:width], in_=banks[Db[0]][:, 0:width]), [("bank", Db[0])], [k0])
            DVE(lambda h: h.reciprocal(out=e1[:, 0:width], in_=banks[Db[1]][:, 0:width]), [("bank", Db[1])], [k1])
            DVE(lambda h: h.tensor_tensor(out=e0[:, 0:width], in0=e0[:, 0:width], in1=banks[Ob[0]][:, 0:width], op=ALU.mult),
                [k0, ("bank", Ob[0])], [k0])
            DVE(lambda h: h.tensor_tensor(out=e1[:, 0:width], in0=e1[:, 0:width], in1=banks[Ob[1]][:, 0:width], op=ALU.mult),
                [k1, ("bank", Ob[1])], [k1])
            DVE(lambda h: h.scalar_tensor_tensor(out=e2[:, 0:width], in0=e1[:, 0:width], scalar=neglam, in1=e0[:, 0:width],
                                                 op0=ALU.mult, op1=ALU.add), [k0, k1, "neglam"], [k2])
            ACT(lambda h: h.activation(out=e3[:, 0:width], in_=e2[:, 0:width], func=AF.Square), [k2], [k3])
            bk = Db[0]
            PE(lambda h: h.matmul(banks[bk][:, 0:width], lhsT=onesF, rhs=e3[:, 0:width], start=True, stop=True),
               [k3, "onesF"], [("bank", bk)])
            ACT(lambda h: h.activation(out=e0[:, 0:width], in_=banks[bk][:, 0:width], func=AF.Sqrt, bias=epsT,
                                       scale=1.0 / 128), [("bank", bk), "epsT"], [k0])
            DVE(lambda h: h.reciprocal(out=e1[:, 0:width], in_=e0[:, 0:width]), [k0], [k1])
            DVE(lambda h: h.scalar_tensor_tensor(out=out_ap, in0=e2[:, 0:width], scalar=gsub, in1=e1[:, 0:width],
                                                 op0=ALU.mult, op1=ALU.mult), [k2, k1, "gsub"], out_keys)

        def prompt_a():
            for hd in range(8):
                b = hd % 2
                DMA_S(lambda h, hd=hd, b=b: h.dma_start(out=kT[b], in_=kaT_s[hd]), allk(True), [("kT", b)], "kT%d" % b)
                DMA_S(lambda h, hd=hd, b=b: h.dma_start(out=vv[b], in_=va_s[hd]), allv(True), [("vv", b)], "vv%d" % b)
                for qt in range(2):
                    q0 = 512 * qt
                    nkt = 24 + 4 * qt + 4
                    O = (4, 5); Dn = (6, 7)
                    def front(kt, hd=hd, b=b, qt=qt, q0=q0):
                        j = kt - (24 + 4 * qt)
                        c0 = 128 * j if j >= 0 else 0
                        sb = (0, 1) if kt % 2 == 0 else (2, 3)
                        for m in range(2):
                            pi = (kt % 3) * 2 + m
                            Pm = Pt[pi]
                            ps = slice(64 * m, 64 * m + 64)
                            PE(lambda h, ps=ps, kt=kt, c0=c0, sbm=sb[m]: h.matmul(
                                banks[sbm][:, c0:512], lhsT=kT[b][ps, kt * 128:(kt + 1) * 128],
                                rhs=qa[ps, hd, q0 + c0:q0 + 512], start=True, stop=True),
                               [("kT", b), ("qa", hd, qt)], [("bank", sb[m])])
                            if kt < 24:
                                sidx = kt // 8
                                ACT(lambda h, Pm=Pm, sbm=sb[m], sidx=sidx: h.activation(
                                    out=Pm[:, 0:512], in_=banks[sbm][:, 0:512], func=AF.Exp, scale=SCL_A,
                                    bias=kbias[:, sidx:sidx + 1]), [("bank", sb[m]), "kbias"], [("P", pi)])
                            else:
                                ACT(lambda h, Pm=Pm, sbm=sb[m], c0=c0: h.activation(
                                    out=Pm[:, c0:512], in_=banks[sbm][:, c0:512], func=AF.Exp, scale=SCL_A),
                                    [("bank", sb[m])], [("P", pi)])
                            if j >= 0:
                                DVE(lambda h, Pm=Pm, c0=c0: h.memset(Pm[64:128, c0:c0 + 64], 0.0), [("P", pi)], [("P", pi)])

                    def back(kt, b=b, qt=qt, nkt=nkt):
                        j = kt - (24 + 4 * qt)
                        c0 = 128 * j if j >= 0 else 0
                        for m in range(2):
                            pi = (kt % 3) * 2 + m
                            Pm = Pt[pi]
                            PE(lambda h, Pm=Pm, m=m, kt=kt, c0=c0: h.matmul(
                                banks[O[m]][:, c0:512], lhsT=vv[b][:, kt, :], rhs=Pm[:, c0:512],
                                start=(kt == 0), stop=(kt == nkt - 1), skip_group_check=True),
                               [("vv", b), ("P", pi)], [("bank", O[m])])
                            PE(lambda h, Pm=Pm, m=m, kt=kt, c0=c0: h.matmul(
                                banks[Dn[m]][:, c0:512], lhsT=onesB, rhs=Pm[:, c0:512],
                                start=(kt == 0), stop=(kt == nkt - 1), skip_group_check=True),
                               [("P", pi), "onesB"], [("bank", Dn[m])])

                    for t in range(nkt + 1):
                        if t < nkt:
                            front(t)
                        if t >= 1:
                            back(t - 1)
                    epilogue_a(O, Dn, 512, qa[:, hd, q0:q0 + 512], [("qa", hd, qt)])

        def prompt_b():
            for hd in range(8):
                b = hd % 2
                DMA_S(lambda h, hd=hd, b=b: h.dma_start(out=kT[b], in_=kbT_s[hd]), allk(False), [("kT", b)], "kT%d" % b)
                DMA_S(lambda h, hd=hd, b=b: h.dma_start(out=vv[b], in_=vb_s[hd]), allv(False), [("vv", b)], "vv%d" % b)
                for qt in range(2):
                    units = []
                    if qt == 0:
                        for s_ in range(3):
                            for t in range(4):
                                units.append((1024 * s_ + 512 + 128 * t, -512 + 128 * t, 0, 128 * t + 128, 3 + s_))
                    for kt in range(8):
                        kstart = 128 * kt
                        a = max(kstart, 512 * qt)
                        e_ = min(kstart + 640, 512 * qt + 512, 1024)
                        if e_ > a:
                            units.append((3072 + 128 * kt, kstart, a, e_, None))
                    OBK, DBK = 4, 5
                    nu = len(units)
                    def frontb(ui, hd=hd, b=b, qt=qt, units=units):
                        kcol, kstart, a, e_, bcol = units[ui]
                        n = e_ - a
                        a_ = a - 512 * qt
                        sbk = ui % 4
                        pi = ui % 6
                        Pm = Pt[pi]
                        ei = ui % 4
                        e = ef[ei]
                        PE(lambda h: h.matmul(
                            banks[sbk][:, a_:a_ + n], lhsT=kT[b][:, kcol:kcol + 128], rhs=qb[:, hd, a:a + n],
                            start=True, stop=True),
                           [("kT", b), ("qb", hd, qt)], [("bank", sbk)])
                        qq0 = a - kstart
                        DVE(lambda h: h.scalar_tensor_tensor(
                            out=e[:, a_:a_ + n], in0=banks[sbk][:, a_:a_ + n], scalar=SCL_B, in1=tb[:, hd, qq0:qq0 + n],
                            op0=ALU.mult, op1=ALU.add), [("bank", sbk), "tb"], [("ef", ei)])
                        if bcol is None:
                            ACT(lambda h: h.activation(out=Pm[:, a_:a_ + n], in_=e[:, a_:a_ + n], func=AF.Exp),
                                [("ef", ei)], [("P", pi)])
                        else:
                            ACT(lambda h: h.activation(
                                out=Pm[:, a_:a_ + n], in_=e[:, a_:a_ + n], func=AF.Exp, bias=kbias[:, bcol:bcol + 1]),
                                [("ef", ei), "kbias"], [("P", pi)])
                        lo, hi_ = max(kstart, a), min(kstart + 64, e_)
                        if hi_ > lo:
                            DVE(lambda h, lo=lo, hi_=hi_: h.memset(Pm[64:128, lo - 512 * qt:hi_ - 512 * qt], 0.0),
                                [("P", pi)], [("P", pi)])
                        lo, hi_ = max(kstart + 576, a), min(kstart + 640, e_)
                        if hi_ > lo:
                            DVE(lambda h, lo=lo, hi_=hi_: h.memset(Pm[0:64, lo - 512 * qt:hi_ - 512 * qt], 0.0),
                                [("P", pi)], [("P", pi)])

                    def backb(ui, b=b, qt=qt, units=units, nu=nu):
                        kcol, kstart, a, e_, bcol = units[ui]
                        n = e_ - a
                        a_ = a - 512 * qt
                        pi = ui % 6
                        Pm = Pt[pi]
                        PE(lambda h: h.matmul(
                            banks[OBK][:, a_:a_ + n], lhsT=vv[b][:, kcol // 128, :], rhs=Pm[:, a_:a_ + n],
                            start=(ui == 0), stop=(ui == nu - 1), skip_group_check=True),
                           [("vv", b), ("P", pi)], [("bank", OBK)])
                        PE(lambda h: h.matmul(
                            banks[DBK][:, a_:a_ + n], lhsT=onesB, rhs=Pm[:, a_:a_ + n],
                            start=(ui == 0), stop=(ui == nu - 1), skip_group_check=True),
                           [("P", pi), "onesB"], [("bank", DBK)])

                    for t in range(nu + 2):
                        if t < nu:
                            frontb(t)
                        if t >= 2:
                            backb(t - 2)
                    ei = ecnt[0] % 4; ecnt[0] += 1
                    e = ef[ei]
                    DVE(lambda h, e=e: h.reciprocal(out=e, in_=banks[DBK][:, :]), [("bank", DBK)], [("ef", ei)])
                    DVE(lambda h, e=e, hd=hd, qt=qt: h.tensor_tensor(out=qb[:, hd, 512 * qt:512 * qt + 512], in0=e,
                                                                    in1=banks[OBK][:, :], op=ALU.mult),
                        [("ef", ei), ("bank", OBK)], [("qb", hd, qt)])

        def sample_all():
            kc = [t_.rearrange("p (k f) -> p k f", k=4) for t_ in kT]
            vc = [t_.rearrange("p k d -> p (k d)").rearrange("p (k f) -> p k f", k=4) for t_ in vv]
            kTs = [ar.bf(1024).rearrange("p (h k) -> p h k", h=8) for _ in range(2)]
            Ps = [ar.bf(1024) for _ in range(2)]
            qbd = ar.bf(1024).rearrange("p (h c) -> p h c", h=8)
            sa = [ar.f32(512) for _ in range(2)]
            oh = ar.f32(512)
            DVE(lambda h: h.memset(qbd, 0.0), [], ["qbd"])
            TRB, SB0, SB1, OB0, OB1, DB0, DB1 = 0, 1, 2, 3, 4, 5, 6
            qk = [("qa", hd, 0) for hd in range(8)]
            qkb = [("qb", hd, 0) for hd in range(8)]
            cnt_s = [0]

            def sample_tiles(bi, is_a, ckey, cval, nkt_cache):
                qc = slice(64 * bi, 64 * bi + 64)
                nk = nkt_cache + 1
                slot_of = {}
                for kt in range(nkt_cache):
                    if kt % 4 == 0:
                        cur = cnt_s[0] % 2; cnt_s[0] += 1
                    slot_of[kt] = cur
                slot_of[nkt_cache] = cur

                def T(kt):
                    if kt == nkt_cache:
                        return
                    sl = slot_of[kt]
                    if kt % 4 == 0:
                        DMA_P(lambda h: h.dma_start(
                            out=kc[sl], in_=ckey[bi, kt * 128:(kt + 4) * 128, :].rearrange("(k p) f -> p k f", p=128)),
                            [], [("kT", sl)], "kc%d" % sl)
                        DMA_P(lambda h: h.dma_start(
                            out=vc[sl], in_=cval[bi, kt * 128:(kt + 4) * 128, :].rearrange("(k p) f -> p k f", p=128)),
                            [], [("vv", sl)], "vc%d" % sl)
                    ks = kt % 2
                    trb = banks[TRB][:].bitcast(BF16).rearrange("p (h k) -> p h k", h=8)
                    for hd in range(8):
                        PE(lambda h, hd=hd: h.transpose(
                            out=trb[:, hd, :], in_=kc[sl][:, kt % 4, hd * 128:(hd + 1) * 128], identity=identB),
                           [("kT", sl), "identB"], [("bank", TRB)])
                    DVE(lambda h: h.tensor_copy(out=kTs[ks], in_=trb), [("bank", TRB)], [("kTs", ks)])

                def Fr(kt):
                    new = (kt == nkt_cache)
                    kp = 64 if new else 128
                    ks = kt % 2
                    ps_ = Ps[ks]
                    if is_a:
                        for hd in range(8):
                            sbk = SB0 if hd < 4 else SB1
                            lhs = (ksTa[:, hd, qc] if new else kTs[ks][:, hd, :])
                            PE(lambda h, hd=hd, sbk=sbk, lhs=lhs: h.matmul(
                                banks[sbk][0:kp, (hd % 4) * 128:(hd % 4 + 1) * 128], lhsT=lhs, rhs=qbd[:, hd, :],
                                start=True, stop=True, skip_group_check=True),
                               [("kTs", ks), "qbd", ("kst", True, hd)], [("bank", sbk)])
                        for half, sbk in ((0, SB0), (1, SB1)):
                            ACT(lambda h, half=half, sbk=sbk: h.activation(
                                out=ps_[0:kp, half * 512:(half + 1) * 512], in_=banks[sbk][0:kp, :], func=AF.Exp, scale=SCL_A),
                                [("bank", sbk)], [("Ps", ks, half)])
                    else:
                        qq0 = 0 if new else 512 - 128 * kt
                        for hd in range(8):
                            lhs = (ksTb[:, hd, qc] if new else kTs[ks][:, hd, :])
                            PE(lambda h, hd=hd, lhs=lhs: h.matmul(
                                banks[SB0][0:kp, hd * 64:(hd + 1) * 64], lhsT=lhs, rhs=qb[:, hd, qc],
                                start=True, stop=True, skip_group_check=True),
                               [("kTs", ks), ("qb", hd, 0), ("kst", False, hd)], [("bank", SB0)])
                        ei = ecnt[0] % 4; ecnt[0] += 1
                        e = ef[ei]
                        DVE(lambda h: h.scalar_tensor_tensor(
                            out=e[0:kp, :].rearrange("p (h q) -> p h q", h=8),
                            in0=banks[SB0][0:kp, :].rearrange("p (h q) -> p h q", h=8), scalar=SCL_B,
                            in1=tb[0:kp, :, qq0:qq0 + 64], op0=ALU.mult, op1=ALU.add),
                            [("bank", SB0), "tb"], [("ef", ei)])
                        ACT(lambda h: h.activation(out=ps_[0:kp, 0:512], in_=e[0:kp, :], func=AF.Exp),
                            [("ef", ei)], [("Ps", ks, 0)])

                def Bk(kt):
                    new = (kt == nkt_cache)
                    kp = 64 if new else 128
                    ks = kt % 2
                    sl = slot_of[kt]
                    ps_ = Ps[ks]
                    if is_a:
                        for hd in range(8):
                            obk = OB0 if hd < 4 else OB1
                            lhs = (vsa[0:64, bi, hd * 128:(hd + 1) * 128] if new else vc[sl][:, kt % 4, hd * 128:(hd + 1) * 128])
                            PE(lambda h, hd=hd, obk=obk, lhs=lhs: h.matmul(
                                banks[obk][:, (hd % 4) * 128:(hd % 4 + 1) * 128], lhsT=lhs,
                                rhs=ps_[0:kp, hd * 128:(hd + 1) * 128],
                                start=(kt == 0 and hd % 4 == 0), stop=(kt == nkt_cache), skip_group_check=True),
                               [("vv", sl), ("Ps", ks, hd // 4), ("vsm", True, 2 * (hd // 2), bi)], [("bank", obk)])
                        for half, dbk in ((0, DB0), (1, DB1)):
                            PE(lambda h, half=half, dbk=dbk: h.matmul(
                                banks[dbk][:, :], lhsT=onesB[0:kp, :], rhs=ps_[0:kp, half * 512:(half + 1) * 512],
                                start=(kt == 0), stop=(kt == nkt_cache), skip_group_check=True),
                               [("Ps", ks, half), "onesB"], [("bank", dbk)])
                    else:
                        for hd in range(8):
                            lhs = (vsb[0:64, bi, hd * 128:(hd + 1) * 128] if new else vc[sl][:, kt % 4, hd * 128:(hd + 1) * 128])
                            PE(lambda h, hd=hd, lhs=lhs: h.matmul(
                                banks[OB0][:, hd * 64:(hd + 1) * 64], lhsT=lhs, rhs=ps_[0:kp, hd * 64:(hd + 1) * 64],
                                start=(kt == 0 and hd == 0), stop=(kt == nkt_cache), skip_group_check=True),
                               [("vv", sl), ("Ps", ks, 0), ("vsm", False, 2 * (hd // 2), bi)], [("bank", OB0)])
                        PE(lambda h: h.matmul(
                            banks[DB0][:, :], lhsT=onesB[0:kp, :], rhs=ps_[0:kp, 0:512],
                            start=(kt == 0), stop=(kt == nkt_cache), skip_group_check=True),
                           [("Ps", ks, 0), "onesB"], [("bank", DB0)])

                for t in range(nk + 2):
                    if t < nk:
                        T(t)
                    if 1 <= t <= nk:
                        Fr(t - 1)
                    if t >= 2:
                        Bk(t - 2)

            for bi in range(4):
                qc = slice(64 * bi, 64 * bi + 64)
                DVE(lambda h, qc=qc: h.tensor_copy(out=qbd[0:64, :, 0:64], in_=qa[0:64, :, qc]), qk + ["qbd"], ["qbd"])
                DVE(lambda h, qc=qc: h.tensor_copy(out=qbd[64:128, :, 64:128], in_=qa[64:128, :, qc]), qk + ["qbd"], ["qbd"])
                sample_tiles(bi, True, cak, cav, 16)
                for half, (obk, dbk) in enumerate(((OB0, DB0), (OB1, DB1))):
                    s_ = sa[half]
                    DVE(lambda h, s_=s_, dbk=dbk: h.reciprocal(out=s_, in_=banks[dbk][:, :]), [("bank", dbk)], [("sa", half)])
                    DVE(lambda h, s_=s_, obk=obk: h.tensor_tensor(out=s_, in0=s_, in1=banks[obk][:, :], op=ALU.mult),
                        [("sa", half), ("bank", obk)], [("sa", half)])
                    sv = s_.rearrange("p (h m q) -> p h m q", h=4, m=2)
                    DVE(lambda h, sv=sv, half=half: h.scalar_tensor_tensor(
                        out=oh[:, half * 256:(half + 1) * 256].rearrange("p (h q) -> p h q", h=4), in0=sv[:, :, 1, :],
                        scalar=neglam, in1=sv[:, :, 0, :], op0=ALU.mult, op1=ALU.add),
                        [("sa", half), "neglam"], [("oh", half)])
                e3 = ef[0]; e0 = ef[1]; e1 = ef[2]
                ACT(lambda h: h.activation(out=e3, in_=oh, func=AF.Square), [("oh", 0), ("oh", 1)], [("ef", 0)])
                PE(lambda h: h.matmul(banks[DB0][:, :], lhsT=onesF, rhs=e3, start=True, stop=True), [("ef", 0), "onesF"],
                   [("bank", DB0)])
                ACT(lambda h: h.activation(out=e0, in_=banks[DB0][:, :], func=AF.Sqrt, bias=epsT, scale=1.0 / 128),
                    [("bank", DB0), "epsT"], [("ef", 1)])
                DVE(lambda h: h.reciprocal(out=e1, in_=e0), [("ef", 1)], [("ef", 2)])
                DVE(lambda h, qc=qc: h.scalar_tensor_tensor(
                    out=qa[:, :, qc], in0=oh.rearrange("p (h q) -> p h q", h=8), scalar=gsub,
                    in1=e1.rearrange("p (h q) -> p h q", h=8), op0=ALU.mult, op1=ALU.mult),
                    [("oh", 0), ("oh", 1), ("ef", 2), "gsub", "qbd"], qk)
                sample_tiles(bi, False, cbk, cbv, 4)
                ei = ecnt[0] % 4; ecnt[0] += 1
                e = ef[ei]
                DVE(lambda h, e=e: h.reciprocal(out=e, in_=banks[DB0][:, :]), [("bank", DB0)], [("ef", ei)])
                DVE(lambda h, e=e, qc=qc: h.tensor_tensor(out=qb[:, :, qc], in0=e.rearrange("p (h q) -> p h q", h=8),
                                                         in1=banks[OB0][:, :].rearrange("p (h q) -> p h q", h=8), op=ALU.mult),
                    [("ef", ei), ("bank", OB0)], qkb)

        if P.kind == "own":
            prompt_a()
            S.barrier()
            prompt_b()
        else:
            sample_all()

    def merge(subs):
        ar = phase_arena(True)
        hbuf = ar.bf(NCH * TT).rearrange("p (c t) -> p c t", c=NCH)
        apply_norm(hbuf, rstd_mix, "rstd_mix", 1, subs)
        NS = 2
        wga = [ar.bf(NCH * 128).rearrange("p (k n) -> p k n", k=NCH) for _ in range(NS)]
        wgb = [ar.bf(NCH * 128).rearrange("p (k n) -> p k n", k=NCH) for _ in range(NS)]
        wa = [ar.bf(8 * 128).rearrange("p (k n) -> p k n", k=8) for _ in range(NS)]
        wb = [ar.bf(8 * 128).rearrange("p (k n) -> p k n", k=8) for _ in range(NS)]
        wo = [ar.bf(2 * D).rearrange("p (k n) -> p k n", k=2) for _ in range(1)]
        mrg = [ar.bf(TT) for _ in range(4)]
        sga = [ar.f32(512) for _ in range(1)]
        sgb = [ar.f32(512) for _ in range(1)]
        ucnt = [0]
        wcnt = [0]
        for M in range(NCH // 2):
            ws = 0
            load_w(wout[256 * M:256 * (M + 1), :], wo[ws], ("wo", ws), 2, wscr["wo"][M], wname="mrg")
            for m in (2 * M, 2 * M + 1):
                sl = m % NS
                load_w(win[:, C_GA + 128 * m:C_GA + 128 * (m + 1)], wga[sl], ("wga", sl), NCH, wscr["wga"][m], wname="mrg")
                load_w(win[:, C_GB + 128 * m:C_GB + 128 * (m + 1)], wgb[sl], ("wgb", sl), NCH, wscr["wgb"][m], wname="mrg")
                load_w(wba[:, 128 * m:128 * (m + 1)], wa[sl], ("wa", sl), 8, wscr["wa"][m], wname="mrg")
                load_w(wbb[:, 128 * m:128 * (m + 1)], wb[sl], ("wb", sl), 8, wscr["wb"][m], wname="mrg")
                ms = m % 4
                for si, (c0, n) in enumerate(subs):
                    u = ucnt[0]; ucnt[0] += 1
                    bga, bgb, bba, bbb = (0, 1, 2, 3) if u % 2 == 0 else (4, 5, 6, 7)
                    sa_, sb_ = sga[0], sgb[0]
                    ka_, kb_ = ("sga", 0), ("sgb", 0)
                    for k in range(NCH):
                        PE(lambda h, k=k, sl=sl, c0=c0, n=n, bga=bga: h.matmul(
                            banks[bga][:, 0:n], lhsT=wga[sl][:, k, :], rhs=hbuf[:, k, c0:c0 + n],
                            start=(k == 0), stop=(k == NCH - 1)), [("wga", sl), ("h", k, si)], [("bank", bga)])
                    for k in range(NCH):
                        PE(lambda h, k=k, sl=sl, c0=c0, n=n, bgb=bgb: h.matmul(
                            banks[bgb][:, 0:n], lhsT=wgb[sl][:, k, :], rhs=hbuf[:, k, c0:c0 + n],
                            start=(k == 0), stop=(k == NCH - 1)), [("wgb", sl), ("h", k, si)], [("bank", bgb)])
                    for k in range(8):
                        PE(lambda h, k=k, sl=sl, c0=c0, n=n, bba=bba: h.matmul(
                            banks[bba][:, 0:n], lhsT=wa[sl][:, k, :], rhs=qa[:, k, c0:c0 + n],
                            start=(k == 0), stop=(k == 7)), [("wa", sl), ("qa", k, si)], [("bank", bba)])
                    for k in range(8):
                        PE(lambda h, k=k, sl=sl, c0=c0, n=n, bbb=bbb: h.matmul(
                            banks[bbb][:, 0:n], lhsT=wb[sl][:, k, :], rhs=qb[:, k, c0:c0 + n],
                            start=(k == 0), stop=(k == 7)), [("wb", sl), ("qb", k, si)], [("bank", bbb)])
                    ACT(lambda h, n=n, sa_=sa_, bga=bga: h.activation(out=sa_[:, 0:n], in_=banks[bga][:, 0:n], func=AF.Sigmoid),
                        [("bank", bga)], [ka_])
                    ACT(lambda h, n=n, sb_=sb_, bgb=bgb: h.activation(out=sb_[:, 0:n], in_=banks[bgb][:, 0:n], func=AF.Sigmoid),
                        [("bank", bgb)], [kb_])
                    DVE(lambda h, n=n, sa_=sa_, bba=bba: h.tensor_tensor(out=sa_[:, 0:n], in0=sa_[:, 0:n], in1=banks[bba][:, 0:n],
                                                                        op=ALU.mult), [ka_, ("bank", bba)], [ka_])
                    DVE(lambda h, n=n, sb_=sb_, bbb=bbb: h.tensor_tensor(out=sb_[:, 0:n], in0=sb_[:, 0:n], in1=banks[bbb][:, 0:n],
                                                                        op=ALU.mult), [kb_, ("bank", bbb)], [kb_])
                    DVE(lambda h, ms=ms, c0=c0, n=n, sa_=sa_, sb_=sb_: h.tensor_tensor(
                        out=mrg[ms][:, c0:c0 + n], in0=sa_[:, 0:n], in1=sb_[:, 0:n], op=ALU.add), [ka_, kb_], [("mrg", ms, si)])
            for nn in range(NCH):
                for si, (c0, n) in enumerate(subs):
                    bk = wcnt[0] % 8; wcnt[0] += 1
                    for kk, m in enumerate((2 * M, 2 * M + 1)):
                        PE(lambda h, bk=bk, ws=ws, nn=nn, m=m, kk=kk, c0=c0, n=n: h.matmul(
                            banks[bk][:, 0:n], lhsT=wo[ws][:, kk, nn * 128:(nn + 1) * 128], rhs=mrg[m % 4][:, c0:c0 + n],
                            start=(kk == 0), stop=(kk == 1)), [("wo", ws), ("mrg", m % 4, si)], [("bank", bk)])
                    DVE(lambda h, bk=bk, nn=nn, c0=c0, n=n: h.tensor_tensor(
                        out=x[:, nn, c0:c0 + n], in0=banks[bk][:, 0:n], in1=x[:, nn, c0:c0 + n], op=ALU.add),
                        [("bank", bk), ("x", nn, si)], [("x", nn, si)])

    def final_out(P):
        ar = phase_arena()
        rstd = ar.f32(TT)
        rms_norm_stats(ar, rstd, "rstd", P.subs)
        yt = [ar.f32(NCH * 128).rearrange("p (c t) -> p c t", c=NCH) for _ in range(2)]
        stg = [ar.f32(D) for _ in range(2)]
        for g, (c0, rows) in enumerate(P.tgs):
            y_ = yt[g % 2]
            si = P.si_of(c0)
            for c in range(NCH):
                DVE(lambda h, y_=y_, c=c, c0=c0, rows=rows: h.scalar_tensor_tensor(
                    out=y_[:, c, 0:rows], in0=x[:, c, c0:c0 + rows], scalar=gsb[:, 3, c:c + 1],
                    in1=rstd[:, c0:c0 + rows], op0=ALU.mult, op1=ALU.mult),
                    [("x", c, si), ("rstd", si), ("gsb", 3)], [("yt", g % 2, c)])
            s = stg[g % 2]
            for q in range(4):
                bk = next_bank(list(range(8)))
                for j in range(4):
                    c = q * 4 + j
                    PE(lambda h, y_=y_, c=c, bk=bk, j=j, rows=rows: h.transpose(
                        out=banks[bk][0:rows, j * 128:(j + 1) * 128], in_=y_[:, c, 0:rows], identity=identF),
                       [("yt", g % 2, c), "identF"], [("bank", bk)])
                if q % 2 == 0:
                    ACT(lambda h, s=s, q=q, bk=bk, rows=rows: h.copy(out=s[0:rows, q * 512:(q + 1) * 512],
                                                                    in_=banks[bk][0:rows, :]),
                        [("bank", bk)], [("ystg", g % 2, q)])
                else:
                    DVE(lambda h, s=s, q=q, bk=bk, rows=rows: h.tensor_copy(out=s[0:rows, q * 512:(q + 1) * 512],
                                                                           in_=banks[bk][0:rows, :]),
                        [("bank", bk)], [("ystg", g % 2, q)])
            dst = yp[c0:c0 + rows, :] if P.kind == "own" else ys[g]
            DMA_A(lambda h, s=s, rows=rows, dst=dst: h.dma_start(out=dst, in_=s[0:rows, :]),
                  [("ystg", g % 2, q) for q in range(4)], [], "yout%d" % (g % 2))
            out_streams.add("yout%d" % (g % 2))

    init_consts()
    init_tbx()
    S.barrier()
    passes = [Pass("prefix", 0), Pass("prefix", 1), Pass("prefix", 2), Pass("own"), Pass("sample")]
    for P in passes:
        WMODE.clear()
        if P.kind == "prefix":
            WMODE.update({"f1": "save", "wi_kv": "save"} if P.idx == 0 else {"f1": "scr", "wi_kv": "scr"})
        elif P.kind == "own":
            WMODE.update({"f1": "scr", "wi_kv": "scr", "wi_q": "save", "mrg": "save", "f2": "save"})
        else:
            WMODE.update({"f1": "scr", "wi_kv": "scr", "wi_q": "scr", "mrg": "scr", "f2": "scr"})
        load_x(P)
        S.barrier()
        ffn(f1g, f1u, f1d, 0, P.subs, "f1")
        S.barrier()
        proj_qkv(P)
        S.barrier()
        if P.kind != "prefix":
            attention(P)
            S.barrier()
            merge(P.subs)
            S.barrier()
            ffn(f2g, f2u, f2d, 2, P.subs, "f2")
            S.barrier()
            final_out(P)
        S.barrier(new_epoch=True)

    es = S.emit(nc, sorted(out_streams))
    es.close()
    stack.close()
    return nc


_NC_CACHE = {}


def _rope(pos):
    half = 32
    inv = (10000.0 ** (-np.arange(half, dtype=np.float32) / half)).astype(np.float32)
    pos = np.asarray(pos, np.float32)
    d = np.arange(128) % 64
    ang = pos[None, :] * inv[d % 32][:, None]
    cos = np.cos(ang).astype(np.float32)
    sin = (np.sin(ang) * np.where(d < 32, -1.0, 1.0)[:, None]).astype(np.float32)
    return np.ascontiguousarray(cos), np.ascontiguousarray(sin)


def kernel(x_prompt, x_sample, cache_a_k, cache_a_v, cache_b_k, cache_b_v,
           ffn1_norm, ffn1_w_gate, ffn1_w_up, ffn1_w_down,
           mix_norm, w_in, lambda_q1, lambda_k1, lambda_q2, lambda_k2, subln_a,
           rel_bias_b, w_branch_a, w_branch_b, w_out,
           ffn2_norm, ffn2_w_gate, ffn2_w_up, ffn2_w_down, final_norm):
    f = lambda a: np.ascontiguousarray(np.asarray(a, dtype=np.float32))
    if "nc" not in _NC_CACHE:
        _NC_CACHE["nc"] = build_program()
    nc = _NC_CACHE["nc"]
    xp_all = f(x_prompt); xs_all = f(x_sample)
    cak = f(cache_a_k)[0].reshape(32, 2048, 1024); cav = f(cache_a_v)[0].reshape(32, 2048, 1024)
    cbk = f(cache_b_k)[0].reshape(32, 512, 1024); cbv = f(cache_b_v)[0].reshape(32, 512, 1024)
    gl = [f(ffn1_norm)[0], f(mix_norm)[0], f(ffn2_norm)[0], f(final_norm)]
    cp = [_rope(np.arange(1024 * s, 1024 * (s + 1))) for s in range(3)]
    cS, sS = _rope(2048 + (np.arange(256) % 64))
    shared = {
        "f1g": f(ffn1_w_gate)[0], "f1u": f(ffn1_w_up)[0], "f1d": f(ffn1_w_down)[0],
        "f2g": f(ffn2_w_gate)[0], "f2u": f(ffn2_w_up)[0], "f2d": f(ffn2_w_down)[0],
        "win": f(w_in)[0], "wba": f(w_branch_a)[0], "wbb": f(w_branch_b)[0], "wout": f(w_out)[0],
        "lamv": np.ascontiguousarray(np.broadcast_to(
            np.stack([f(lambda_q1)[0], f(lambda_k1)[0], f(lambda_q2)[0], f(lambda_k2)[0]])[:, None, :], (4, 128, 64))),
        "subln": f(subln_a)[0].reshape(128, 1), "relb": f(rel_bias_b)[0],
        "cosP": np.stack([c for c, _ in cp]), "sinP": np.stack([s_ for _, s_ in cp]),
        "cosS": cS, "sinS": sS, "ident_in": np.eye(128, dtype=np.float32),
    }
    for g in range(4):
        shared["gain%d" % g] = np.ascontiguousarray(gl[g].reshape(NCH, 128).T)
    in_maps = []
    for c in range(NCORES):
        seq, r = c // 4, c % 4
        m = dict(shared)
        m["xpre"] = xp_all[seq, 0:3072].reshape(3, 1024, D)
        m["xown"] = xp_all[seq, 1024 * r:1024 * (r + 1)]
        m["xs"] = xs_all[4 * c:4 * c + 4]
        m["cak"] = cak[4 * c:4 * c + 4]; m["cav"] = cav[4 * c:4 * c + 4]
        m["cbk"] = cbk[4 * c:4 * c + 4]; m["cbv"] = cbv[4 * c:4 * c + 4]
        m["cosO"], m["sinO"] = _rope(np.arange(1024 * r, 1024 * (r + 1)))
        kb = np.full((128, 8), NEG_BIG, np.float32)
        for s_ in range(3):
            if s_ < r:
                kb[:, s_] = 0.0
            if s_ == r - 1:
                kb[:, 3 + s_] = 0.0
        m["kbias"] = kb
        in_maps.append(m)
    res = run_bass_kernel_spmd(nc, in_maps, core_ids=list(range(NCORES)))
    R = res.results
    cat = lambda key, seq: np.concatenate([R[4 * seq + r][key] for r in range(4)], axis=0)
    y_prompt = np.stack([cat("yp", 0), cat("yp", 1)]).reshape(2, 4096, D)
    y_sample = np.concatenate([R[c]["ys"] for c in range(NCORES)]).reshape(32, 64, D)
    akp = np.stack([cat("akp", 0), cat("akp", 1)]).reshape(1, 2, 4096, 8, 128)
    avp = np.stack([cat("avp", 0), cat("avp", 1)]).reshape(1, 2, 4096, 8, 128)
    bkp = np.stack([R[3]["bkp"], R[7]["bkp"]]).reshape(1, 2, 512, 8, 128)
    bvp = np.stack([R[3]["bvp"], R[7]["bvp"]]).reshape(1, 2, 512, 8, 128)
    aks = np.concatenate([R[c]["aks"] for c in range(NCORES)]).reshape(1, 32, 64, 8, 128)
    avs = np.concatenate([R[c]["avs"] for c in range(NCORES)]).reshape(1, 32, 64, 8, 128)
    bks = np.concatenate([R[c]["bks"] for c in range(NCORES)]).reshape(1, 32, 512, 8, 128)
    bvs = np.concatenate([R[c]["bvs"] for c in range(NCORES)]).reshape(1, 32, 512, 8, 128)
    return (y_prompt.astype(np.float32), y_sample.astype(np.float32), akp, avp, bkp, bvp, aks, avs, bks, bvs)
```
